# Optimizing a Trainium2 kernel written in Bass

```python
import math
import jax, jax.numpy as jnp
from jax import lax
import numpy as np

D_MODEL = 2048
BATCH = 2
SEQ = 4096
DEPTH = 2
DEC_BATCH = 32
DEC_SEQ = 4
PAST_LEN = 16384
PAGE_SIZE = 128

GROUP_W = D_MODEL // 4
D_MIX = 4 * GROUP_W
CONV_W = 3
CONV_GROUPS = 4
GLA_HEADS = 4
GLA_DK = GROUP_W // 2 // GLA_HEADS
GLA_DV = GROUP_W // GLA_HEADS
GLA_RANK = 16
GLA_TAU = 16.0
GLA_CHUNK = 64
SWA_HEADS = 8
SWA_KV_HEADS = 2
SWA_HD = GROUP_W // SWA_HEADS
WINDOW = 128
SWA_BLOCK = WINDOW
N_MEM = 256
MEM_HEADS = 4
MEM_HD = GROUP_W // MEM_HEADS
EPS = 1e-6

IN_SPLITS = (GROUP_W, GROUP_W, GROUP_W, GROUP_W,
             GLA_HEADS * GLA_DK, GLA_HEADS * GLA_DK, GLA_HEADS * GLA_DV, GLA_RANK, GROUP_W,
             SWA_HEADS * SWA_HD, SWA_KV_HEADS * SWA_HD, SWA_KV_HEADS * SWA_HD, GROUP_W,
             MEM_HEADS * MEM_HD, GROUP_W)
IN_OFFSETS = tuple(sum(IN_SPLITS[:i + 1]) for i in range(len(IN_SPLITS) - 1))
D_IN = sum(IN_SPLITS)

kernel_name = "hymba_conv_gla_swa_memory_step"


def rmsnorm(x, g):
    xf = x.astype(jnp.float32)
    y = xf * lax.rsqrt(jnp.mean(xf * xf, axis=-1, keepdims=True) + EPS)
    return (y * g.astype(jnp.float32)).astype(x.dtype)


def short_conv(u, prev, w):
    L = u.shape[1]
    ext = jnp.concatenate([prev.astype(u.dtype), u], axis=1)
    out = w[0] * ext[:, 0:L]
    for j in range(1, CONV_W):
        out = out + w[j] * ext[:, j:j + L]
    return out, ext[:, -(CONV_W - 1):]


def gla_chunked(q, k, v, log_a, s0, chunk):
    b, L, H, _ = q.shape
    n = L // chunk

    def to_chunks(t):
        return t.reshape(b, n, chunk, H, t.shape[-1]).transpose(1, 0, 3, 2, 4)

    qc, kc, vc, ac = to_chunks(q), to_chunks(k), to_chunks(v), to_chunks(log_a)
    mask = jnp.tril(jnp.ones((chunk, chunk), dtype=bool))

    def step(S, inp):
        qi, ki, vi, ai = inp
        cum = jnp.cumsum(ai, axis=2)
        total = cum[:, :, -1:]
        q_dec = qi * jnp.exp(cum)
        k_dec = ki * jnp.exp(-cum)
        attn = jnp.where(mask, jnp.einsum('bhtd,bhsd->bhts', q_dec, k_dec), 0.0)
        o = jnp.einsum('bhts,bhsv->bhtv', attn, vi) + jnp.einsum('bhtd,bhdv->bhtv', q_dec, S)
        k_tail = ki * jnp.exp(total - cum)
        S_new = jnp.exp(total)[:, :, 0, :, None] * S + jnp.einsum('bhsd,bhsv->bhdv', k_tail, vi)
        return S_new, o

    S, o = lax.scan(step, s0, (qc, kc, vc, ac))
    o = o.transpose(1, 0, 3, 2, 4).reshape(b, L, H, -1)
    return o, S


def sink_softmax(s, valid, sinks):
    sink = sinks.astype(jnp.float32).reshape(SWA_KV_HEADS, -1, 1, 1)
    s = jnp.where(valid, s, -jnp.inf)
    m = jnp.maximum(jnp.max(s, axis=-1, keepdims=True), sink)
    e = jnp.exp(s - m)
    return e / (jnp.sum(e, axis=-1, keepdims=True) + jnp.exp(sink - m))


def swa_banded(q, k, v, sinks):
    b, L, H, hd = q.shape
    n = L // SWA_BLOCK
    G = H // SWA_KV_HEADS
    qb = q.reshape(b, n, SWA_BLOCK, SWA_KV_HEADS, G, hd)

    def band(t):
        prev = jnp.concatenate([jnp.zeros_like(t[:, :SWA_BLOCK]), t[:, :-SWA_BLOCK]], axis=1)
        return jnp.concatenate([prev.reshape(b, n, SWA_BLOCK, SWA_KV_HEADS, hd),
                                t.reshape(b, n, SWA_BLOCK, SWA_KV_HEADS, hd)], axis=2)

    kb, vb = band(k), band(v)
    s = jnp.einsum('bnqkgd,bnskd->bnkgqs', qb, kb).astype(jnp.float32) * (hd ** -0.5)
    blk = jnp.arange(n)[:, None, None]
    qi = jnp.arange(SWA_BLOCK)[None, :, None]
    kj = jnp.arange(2 * SWA_BLOCK)[None, None, :]
    dist = qi + SWA_BLOCK - kj
    valid = (dist >= 0) & (dist < WINDOW) & (blk * SWA_BLOCK + kj - SWA_BLOCK >= 0)
    p = sink_softmax(s, valid[:, None, None], sinks)
    o = jnp.einsum('bnkgqs,bnskd->bnqkgd', p.astype(v.dtype), vb)
    return o.reshape(b, L, H, hd)


def swa_decode(q, k_all, v_all, sinks):
    b, T, H, hd = q.shape
    G = H // SWA_KV_HEADS
    S = k_all.shape[1]
    qg = q.reshape(b, T, SWA_KV_HEADS, G, hd)
    s = jnp.einsum('btkgd,bskd->bkgts', qg, k_all).astype(jnp.float32) * (hd ** -0.5)
    dist = jnp.arange(T)[:, None] + (S - T) - jnp.arange(S)[None, :]
    valid = (dist >= 0) & (dist < WINDOW)
    p = sink_softmax(s, valid, sinks)
    o = jnp.einsum('bkgts,bskd->btkgd', p.astype(v_all.dtype), v_all)
    return o.reshape(b, T, H, hd)


def memory_kv(mem, g_mem, w_mem_kv, g_mem_k):
    b, m, _ = mem.shape
    kv = rmsnorm(mem, g_mem) @ w_mem_kv
    k, v = jnp.split(kv, 2, axis=-1)
    k = rmsnorm(k.reshape(b, m, MEM_HEADS, MEM_HD), g_mem_k)
    return k, v.reshape(b, m, MEM_HEADS, MEM_HD)


def mixer_layer(x, mem_k, mem_v, conv_prev, gla_s0, swa_k_prev, swa_v_prev,
                g_norm, w_in, conv_w, w_gla_a_up, b_gla_a, g_gla_o, g_swa_q, g_swa_k,
                swa_sinks, g_mem_q, w_out):
    b, L, _ = x.shape
    f32 = jnp.float32
    hn = rmsnorm(x, g_norm)
    proj = hn @ w_in
    (a_b, a_c, a_h, a_z, g_q, g_k, g_v, g_a, g_z,
     s_q, s_k, s_v, s_z, m_q, m_z) = jnp.split(proj, IN_OFFSETS, axis=-1)

    if conv_prev is None:
        conv_prev = jnp.zeros((b, CONV_W - 1, GROUP_W), x.dtype)
    conv_out, conv_state = short_conv(a_c * a_h, conv_prev, conv_w)
    y_a = a_b * conv_out * jax.nn.silu(a_z)

    q = g_q.reshape(b, L, GLA_HEADS, GLA_DK).astype(f32) * (GLA_DK ** -0.5)
    k = g_k.reshape(b, L, GLA_HEADS, GLA_DK).astype(f32)
    v = g_v.reshape(b, L, GLA_HEADS, GLA_DV).astype(f32)
    log_a = jax.nn.log_sigmoid((g_a @ w_gla_a_up + b_gla_a).astype(f32)).reshape(b, L, GLA_HEADS, GLA_DK) / GLA_TAU
    if gla_s0 is None:
        s0 = jnp.zeros((b, GLA_HEADS, GLA_DK, GLA_DV), f32)
    else:
        s0 = gla_s0.astype(f32)
    chunk = GLA_CHUNK if L % GLA_CHUNK == 0 else L
    o_gla, gla_state = gla_chunked(q, k, v, log_a, s0, chunk)
    o_gla = rmsnorm(o_gla, g_gla_o).astype(x.dtype)
    y_b = o_gla.reshape(b, L, GROUP_W) * jax.nn.silu(g_z)

    q = rmsnorm(s_q.reshape(b, L, SWA_HEADS, SWA_HD), g_swa_q)
    k = rmsnorm(s_k.reshape(b, L, SWA_KV_HEADS, SWA_HD), g_swa_k)
    v = s_v.reshape(b, L, SWA_KV_HEADS, SWA_HD)
    if swa_k_prev is None:
        o_swa = swa_banded(q, k, v, swa_sinks)
        k_buf, v_buf = k[:, -WINDOW:], v[:, -WINDOW:]
    else:
        k_all = jnp.concatenate([swa_k_prev.astype(k.dtype), k], axis=1)
        v_all = jnp.concatenate([swa_v_prev.astype(v.dtype), v], axis=1)
        o_swa = swa_decode(q, k_all, v_all, swa_sinks)
        k_buf, v_buf = k_all[:, -WINDOW:], v_all[:, -WINDOW:]
    y_c = o_swa.reshape(b, L, GROUP_W) * jax.nn.silu(s_z)

    q = rmsnorm(m_q.reshape(b, L, MEM_HEADS, MEM_HD), g_mem_q)
    s = jnp.einsum('blhd,bmhd->bhlm', q, mem_k.astype(q.dtype)).astype(f32) * (MEM_HD ** -0.5)
    p = jax.nn.softmax(s, axis=-1).astype(x.dtype)
    o_mem = jnp.einsum('bhlm,bmhd->blhd', p, mem_v.astype(x.dtype))
    y_d = o_mem.reshape(b, L, GROUP_W) * jax.nn.silu(m_z)

    mix = jnp.concatenate([y_a, y_b, y_c, y_d], axis=-1)
    y = x + mix @ w_out
    return y, conv_state, gla_state.astype(x.dtype), k_buf, v_buf


def setup_inputs(seed: int = 0) -> dict:
    key = jax.random.key(seed)
    ks = jax.random.split(key, 32)

    def nrm(k, shape, scale=1.0):
        return jax.random.normal(k, shape, jnp.float32) * scale

    def gain(k, shape):
        return 1.0 + 0.02 * jax.random.normal(k, shape, jnp.float32)

    return {
        "x_prompt": nrm(ks[0], (BATCH, SEQ, D_MODEL)),
        "x_sample": nrm(ks[1], (DEC_BATCH, DEC_SEQ, D_MODEL)),
        "mem_prompt": nrm(ks[2], (BATCH, N_MEM, D_MODEL)),
        "state_conv": nrm(ks[3], (DEPTH, DEC_BATCH, CONV_W - 1, GROUP_W)),
        "state_gla": nrm(ks[4], (DEPTH, DEC_BATCH, GLA_HEADS, GLA_DK, GLA_DV), 0.3),
        "cache_swa_k": nrm(ks[5], (DEPTH, DEC_BATCH, WINDOW, SWA_KV_HEADS, SWA_HD)),
        "cache_swa_v": nrm(ks[6], (DEPTH, DEC_BATCH, WINDOW, SWA_KV_HEADS, SWA_HD)),
        "cache_mem_k": nrm(ks[7], (DEPTH, DEC_BATCH, N_MEM, MEM_HEADS, MEM_HD)),
        "cache_mem_v": nrm(ks[8], (DEPTH, DEC_BATCH, N_MEM, MEM_HEADS, MEM_HD)),
        "g_norm": gain(ks[9], (DEPTH, D_MODEL)),
        "w_in": nrm(ks[10], (DEPTH, D_MODEL, D_IN), D_MODEL ** -0.5),
        "conv_w": nrm(ks[11], (DEPTH, CONV_W, GROUP_W), CONV_W ** -0.5),
        "w_gla_a_up": nrm(ks[12], (DEPTH, GLA_RANK, GLA_HEADS * GLA_DK), GLA_RANK ** -0.5),
        "b_gla_a": nrm(ks[13], (DEPTH, GLA_HEADS * GLA_DK), 0.1),
        "g_gla_o": gain(ks[14], (DEPTH, GLA_DV)),
        "g_swa_q": gain(ks[15], (DEPTH, SWA_HD)),
        "g_swa_k": gain(ks[16], (DEPTH, SWA_HD)),
        "swa_sinks": nrm(ks[17], (DEPTH, SWA_HEADS), 0.5),
        "g_mem": gain(ks[18], (DEPTH, D_MODEL)),
        "w_mem_kv": nrm(ks[19], (DEPTH, D_MODEL, 2 * MEM_HEADS * MEM_HD), D_MODEL ** -0.5),
        "g_mem_q": gain(ks[20], (DEPTH, MEM_HD)),
        "g_mem_k": gain(ks[21], (DEPTH, MEM_HD)),
        "w_out": nrm(ks[22], (DEPTH, D_MIX, D_MODEL), D_MIX ** -0.5),
    }


def reference(x_prompt, x_sample, mem_prompt, state_conv, state_gla, cache_swa_k, cache_swa_v,
              cache_mem_k, cache_mem_v, g_norm, w_in, conv_w, w_gla_a_up, b_gla_a, g_gla_o,
              g_swa_q, g_swa_k, swa_sinks, g_mem, w_mem_kv, g_mem_q, g_mem_k, w_out):
    hp, hs = x_prompt, x_sample
    conv_p, gla_p, swk_p, swv_p, mk_p, mv_p = [], [], [], [], [], []
    conv_s, gla_s, swk_s, swv_s = [], [], [], []
    for l in range(DEPTH):
        lw = (g_norm[l], w_in[l], conv_w[l], w_gla_a_up[l], b_gla_a[l], g_gla_o[l],
              g_swa_q[l], g_swa_k[l], swa_sinks[l], g_mem_q[l], w_out[l])
        mk, mv = memory_kv(mem_prompt, g_mem[l], w_mem_kv[l], g_mem_k[l])
        hp, c, s, kb, vb = mixer_layer(hp, mk, mv, None, None, None, None, *lw)
        conv_p.append(c); gla_p.append(s); swk_p.append(kb); swv_p.append(vb)
        mk_p.append(mk); mv_p.append(mv)
        hs, c, s, kb, vb = mixer_layer(hs, cache_mem_k[l], cache_mem_v[l], state_conv[l], state_gla[l],
                                       cache_swa_k[l], cache_swa_v[l], *lw)
        conv_s.append(c); gla_s.append(s); swk_s.append(kb); swv_s.append(vb)
    return (hp, hs,
            jnp.stack(conv_p), jnp.stack(gla_p), jnp.stack(swk_p), jnp.stack(swv_p),
            jnp.stack(mk_p), jnp.stack(mv_p),
            jnp.stack(conv_s), jnp.stack(gla_s), jnp.stack(swk_s), jnp.stack(swv_s))
```

```python
import os
import numpy as np
import concourse.bass as bass
import concourse.mybir as mybir
from concourse.bass_utils import run_bass_kernel_spmd

F32 = mybir.dt.float32
BF16 = mybir.dt.bfloat16
ALU = mybir.AluOpType
AF = mybir.ActivationFunctionType
AX = mybir.AxisListType


class Buf:
    __slots__ = ("name", "w", "r")

    def __init__(self, name):
        self.name = name
        self.w = None
        self.r = []


class Ctx:
    ENG = ("pe", "act", "dve", "pool", "sp")

    def __init__(self, nc, n_dma_sems=40, immediate=True):
        self.nc = nc
        self.immediate = immediate
        self.hnd = {"pe": nc.tensor, "act": nc.scalar, "dve": nc.vector, "pool": nc.gpsimd, "sp": nc.sync}
        self.prog = {e: [] for e in self.ENG}
        self.sem = {e: nc.alloc_semaphore("s_" + e) for e in self.ENG}
        self.cnt = {e: 0 for e in self.ENG}
        self.known = {e: {} for e in self.ENG}
        self.dsem = [nc.alloc_semaphore("d%d" % i) for i in range(n_dma_sems)]
        self.dval = [0] * n_dma_sems
        n_pool = n_dma_sems // 2
        self.pool_hist = []
        self.pool_outstanding = int(os.environ.get("KTHROTTLE", "16"))
        self.dpool = {"pool": list(range(0, n_pool)), "sp": list(range(n_pool, n_dma_sems)), "act": list(range(n_pool, n_dma_sems))}
        self.drr = {"pool": 0, "sp": 0, "act": 0}
        self.nbuf = 0

    def buf(self, name=None):
        self.nbuf += 1
        return Buf(name or ("b%d" % self.nbuf))

    def _semh(self, key):
        return self.sem[key[1]] if key[0] == "e" else self.dsem[key[1]]

    def _need(self, eng, deps):
        best = {}
        for key, val in deps:
            if val > best.get(key, 0):
                best[key] = val
        out = []
        kn = self.known[eng]
        for key, val in best.items():
            if kn.get(key, 0) >= val:
                continue
            kn[key] = val
            out.append((key, val))
        return out

    def _deps(self, eng, reads, writes):
        deps = []
        me = ("e", eng)
        for b in reads:
            if b.w is not None:
                deps.append(b.w)
        for b in writes:
            if b.w is not None and not (eng == "pe" and b.w[0] == me):
                deps.append(b.w)
            for r in b.r:
                deps.append(r)
        return deps

    @staticmethod
    def _flat(x):
        out = []
        for b in x:
            if isinstance(b, (list, tuple)):
                out.extend(Ctx._flat(b))
            else:
                out.append(b)
        return out

    def op(self, eng, fn, reads=(), writes=()):
        reads = self._flat(reads); writes = self._flat(writes)
        waits = self._need(eng, self._deps(eng, reads, writes))
        self.cnt[eng] += 1
        val = self.cnt[eng]
        sem = self.sem[eng]
        wl = [(self._semh(k), v) for k, v in waits]

        def emit(h, fn=fn, wl=wl, sem=sem):
            for s, v in wl:
                h.wait_ge(s, v)
            fn(h).then_inc(sem, 1)

        self._push(eng, emit)
        key = ("e", eng)
        for b in reads:
            b.r.append((key, val))
        for b in writes:
            b.w = (key, val)
            b.r = []
        return val

    def dma(self, eng, fn, reads=(), writes=()):
        reads = self._flat(reads); writes = self._flat(writes)
        lst = self.dpool[eng]
        k = "sp" if eng == "act" else eng
        i = lst[self.drr[k]]
        self.drr[k] = (self.drr[k] + 1) % len(lst)
        deps = self._deps(eng, reads, writes)
        key = ("d", i)
        if self.dval[i] > 0:
            deps.append((key, self.dval[i]))
        if eng == "pool" and len(self.pool_hist) >= self.pool_outstanding:
            deps.append(self.pool_hist[-self.pool_outstanding])
        waits = self._need(eng, deps)
        self.dval[i] += 16
        val = self.dval[i]
        sem = self.dsem[i]
        wl = [(self._semh(k), v) for k, v in waits]

        def emit(h, fn=fn, wl=wl, sem=sem):
            for s, v in wl:
                h.wait_ge(s, v)
            fn(h).then_inc(sem, 16)

        self._push(eng, emit)
        if eng == "pool":
            self.pool_hist.append((key, val))
        for b in reads:
            b.r.append((key, val))
        for b in writes:
            b.w = (key, val)
            b.r = []
        return key, val

    def wait_all(self, eng, bufs):
        deps = [b.w for b in bufs if b.w is not None]
        waits = self._need(eng, deps)
        wl = [(self._semh(k), v) for k, v in waits]

        def emit(h, wl=wl):
            for s, v in wl:
                h.wait_ge(s, v)

        self._push(eng, emit)

    def _push(self, eng, emit):
        if self.immediate:
            emit(self.hnd[eng])
        else:
            self.prog[eng].append(emit)

    def finish(self):
        nc = self.nc
        prog = self.prog
        if self.immediate:
            return
        with nc.Block() as block:
            @block.tensor
            def _(h):
                for f in prog["pe"]:
                    f(h)

            @block.scalar
            def _(h):
                for f in prog["act"]:
                    f(h)

            @block.vector
            def _(h):
                for f in prog["dve"]:
                    f(h)

            @block.gpsimd
            def _(h):
                for f in prog["pool"]:
                    f(h)

            @block.sync
            def _(h):
                for f in prog["sp"]:
                    f(h)


D = 2048
NP = 1024
NSQ = 4
NST = 16
NT = NP + NST
DIN = 5904
A_B, A_C, A_H, A_Z = 0, 512, 1024, 1536
G_Q, G_K, G_V, G_A, G_Z = 2048, 2304, 2560, 3072, 3088
S_Q, S_K, S_V, S_Z = 3600, 4112, 4240, 4368
M_Q, M_Z = 4880, 5392
EPS = 1e-6
SEGS = [(0, NP)] + [(NP + 4 * q, 4) for q in range(NSQ)]
TT = [(128 * i, 128) for i in range(8)] + [(NP + 4 * q, 4) for q in range(NSQ)]
PAYC = 528
WORKC = 1044
NSLOT = 4
WSPLIT = 4
SWA_MASK_ENG = os.environ.get("KSWAENG", "dve")


class _Stop(Exception):
    pass


def build(wseq_in=None):
    STAGE = float(os.environ.get("KSTAGE", "99"))

    def stage(n):
        if n == STAGE:
            raise _Stop()
    nc = bass.Bass("TRN2", target_bir_lowering=False)
    c = Ctx(nc, n_dma_sems=72)

    def din(name, shape):
        return nc.dram_tensor(name, list(shape), F32, kind="ExternalInput").ap()

    def dout(name, shape):
        return nc.dram_tensor(name, list(shape), F32, kind="ExternalOutput").ap()

    xp = din("xp", [NP, D]); xs_in = din("xs", [NST, D]); memp = din("memp", [256, D])
    st_conv = din("st_conv", [2, NSQ, 2, 512]); st_gla = din("st_gla", [2, NSQ, 4, 64, 128])
    c_swk = din("c_swk", [2, NSQ, 128, 128]); c_swv = din("c_swv", [2, NSQ, 128, 128])
    c_mk = din("c_mk", [2, NSQ, 256, 512]); c_mv = din("c_mv", [2, NSQ, 256, 512])
    g_norm = din("g_norm", [2, D]); w_in = din("w_in", [2, D, DIN]); conv_w = din("conv_w", [2, 3, 512])
    w_up = din("w_up", [2, 16, 256]); b_a = din("b_a", [2, 256]); g_o = din("g_o", [2, 128])
    g_sq = din("g_sq", [2, 64]); g_sk = din("g_sk", [2, 64]); sinks = din("sinks", [2, 8])
    g_mem = din("g_mem", [2, D]); w_mkv = din("w_mkv", [2, D, 1024]); g_mq = din("g_mq", [2, 128])
    g_mk = din("g_mk", [2, 128]); w_out = din("w_out", [2, D, D]); cmask = din("cmask", [1, 16])

    y_p = dout("y_p", [NP, D]); y_s = dout("y_s", [NST, D])
    o_conv_p = dout("conv_p", [2, 2, 512]); o_gla_p = dout("gla_p", [2, 256, 128])
    o_swk_p = dout("swk_p", [2, 128, 128]); o_swv_p = dout("swv_p", [2, 128, 128])
    o_mk_p = dout("mk_p", [2, 256, 512]); o_mv_p = dout("mv_p", [2, 256, 512])
    o_conv_s = dout("conv_s", [2, NSQ, 2, 512]); o_gla_s = dout("gla_s", [2, NSQ, 256, 128])
    o_swk_s = dout("swk_s", [2, NSQ, 128, 128]); o_swv_s = dout("swv_s", [2, NSQ, 128, 128])
    agin = [nc.dram_tensor("agin%d" % l, [128, PAYC], F32) for l in range(2)]
    agout = [nc.dram_tensor("agout%d" % l, [4 * 128, PAYC], F32) for l in range(2)]
    OUTB = c.buf("outputs")
    AGB = [(c.buf(), c.buf()) for _ in range(2)]

    A = nc.alloc_sbuf_tensor
    x_res = A("x_res", [128, 8, D], F32); XB = [c.buf("x%d" % i) for i in range(8)]
    x_s = A("x_s", [NST, D], F32); XSB = c.buf("x_s")
    hnT = A("hnT", [128, 16, NT], BF16); HB = c.buf("hnT"); HBm = c.buf("hnTm")
    mixT = A("mixT", [128, 4, NT], BF16); MBall = [c.buf("mix%d" % j) for j in range(4)]
    MB = [MBall, MBall]
    mixflat = mixT[:].rearrange("p b n -> p (b n)")
    xsb = [mixflat[:, 0:2048], mixflat[:, 2048:4096]]
    XSBB = [c.buf("xsb0"), c.buf("xsb1")]
    xs_first = {0: True, 1: True}
    wring = A("wring", [128, NSLOT, 16, 128], BF16); WB = [[c.buf("w%d_%d" % (i, j)) for j in range(16)] for i in range(NSLOT)]
    v_tm = A("v_tm", [128, 12, 512], BF16); VB = [c.buf() for _ in range(12)]
    vs_tm = A("vs_tm", [128, 12, 128], BF16); VSB = [c.buf() for _ in range(12)]
    kT_bf = A("kT_bf", [128, 2, NT], BF16); KTB = [c.buf(), c.buf()]
    kn = A("kn", [128, 2, NT], BF16); KNB = [c.buf(), c.buf()]
    Sloc = A("Sloc", [128, 2, 8, 128], BF16); SLB = [[c.buf() for _ in range(8)] for _ in range(2)]
    a_ext = A("a_ext", [17, NT], BF16); AEB = c.buf("a_ext")
    WH = [A("wh%d" % i, [128, WORKC], BF16) for i in range(4)]; WHB = [c.buf("wh%d" % i) for i in range(4)]
    WFall = A("wfall", [128, 3, WORKC], F32)
    WF = [WFall[:, i, :] for i in range(3)]; WFB = [c.buf("wf%d" % i) for i in range(3)]
    memstage = WFall[:].rearrange("p a n -> p (a n)")[:, 0:2048]
    memK = WH[3][:, 0:1024].rearrange("p (a b) -> p a b", a=4); MKB = WHB[3]
    memV = WH[2][:, 0:1024].rearrange("p (a b) -> p a b", a=2); MVB = WHB[2]
    TF = [A("tf%d" % i, [128, PAYC], F32) for i in range(3)]; TFB = [c.buf("tf%d" % i) for i in range(3)]
    TH = [A("th%d" % i, [128, 1024], BF16) for i in range(2)]; THB = [c.buf("th%d" % i) for i in range(2)]
    ident = A("ident", [128, 128], BF16)
    triU = A("triU", [128, 128], F32); triUb = A("triUb", [128, 128], BF16)
    swm = A("swm", [128, 2, 2, 128], BF16); swm1 = A("swm1", [128, 2, 2, 128], BF16)
    onesb = A("onesb", [128, 128], BF16); bdiag = A("bdiag", [128, 128], BF16)
    CB = c.buf("consts")
    gT = A("gT", [128, 2, 16], F32); gmT = A("gmT", [128, 2, 16], F32)
    cw = A("cw", [128, 2, 3, 4], F32); wupb = A("wupb", [17, 2, 256], BF16)
    gov = A("gov", [128, 2], F32); gsq = A("gsq", [128, 2], F32); gsk = A("gsk", [128, 2], F32)
    gmq = A("gmq", [128, 2], F32); gmkb = A("gmkb", [128, 2, 128], F32)
    esk = A("esk", [128, 2, 8], F32); esq = A("esq", [128, 2, 4], F32)
    cm = A("cm", [128, 16], F32)
    PB = c.buf("params")
    sm = A("sm", [128, 64], F32)
    SMB = [c.buf() for _ in range(64)]
    fix_t1 = A("fix_t1", [128, 4, 5, 2], F32); fix_bz = A("fix_bz", [128, 4, 5, 2], F32)
    fix_lk = A("fix_lk", [128, 4, 5, 2], F32); ulast = A("ulast", [128, 4, 5, 2], F32)
    cprev = A("cprev", [128, 4, 5, 2], F32); FXB = c.buf("fix"); CPB = c.buf("cprev")
    S_run = A("S_run", [128, 2, 128], F32); SRB = [c.buf(), c.buf()]
    S_st = A("S_st", [128, 2, 5, 128], F32); SSB = c.buf("S_st")
    eG = A("eG", [128, 2, 9], F32); EGB = c.buf("eG")
    kprev = A("kprev", [128, 2, 5, 128], BF16); KPB = [c.buf() for _ in range(5)]
    vprev = A("vprev", [128, 5, 128], BF16); VPB = [c.buf() for _ in range(5)]
    pay = A("pay", [128, PAYC], F32); PYB = c.buf("pay")
    qn_s = A("qn_s", [128, 4, NST], BF16); sz_s = A("sz_s", [128, 4, NST], BF16); QSB = c.buf("qn_s")
    psA = nc.alloc_psum_tensor("psA", [128, 1024], F32); psB = nc.alloc_psum_tensor("psB", [128, 1024], F32)
    psS = nc.alloc_psum_tensor("psS", [128, 512], F32)
    psT = [nc.alloc_psum_tensor("psT%d" % i, [128, 512], F32) for i in range(3)]
    PAB = [c.buf("psA"), c.buf("psB")]; PSB = [c.buf() for _ in range(32)]; PTB = [c.buf() for _ in range(3)]
    PA = [psA, psB]
    st = {"ps": 0, "pss": 0, "pt": 0, "w": 0, "tf": 0, "th": 0, "sm": 0, "wk": 0, "wi": 0}
    WSEQ = list(wseq_in) if wseq_in is not None else []

    def nxt(k, n):
        v = st[k]; st[k] = (v + 1) % n; return v

    def smcol():
        i = nxt("sm", 64); return sm[:, i:i + 1], SMB[i]

    def materialize(spec):
        kind = spec[0]
        if kind == "win":
            _, l, cols = spec
            wv = w_in[l].rearrange("(kt p) c -> p kt c", p=128)
            dm = []; off = 0
            for c0, n in cols:
                dm.append((lambda sl, off=off, n=n: sl[:, :, off:off + n], wv[:, :, c0:c0 + n])); off += n
            return dm
        if kind == "mkv":
            _, l, j = spec
            wv = w_mkv[l].rearrange("(kt p) c -> p kt c", p=128)
            return [(lambda sl: sl, wv[:, :, 128 * j:128 * j + 128])]
        _, l, g, half = spec
        src = w_out[l][g * 512:(g + 1) * 512, half * 512:(half + 1) * 512].rearrange("(kt p) c -> p kt c", p=128)
        return [(lambda sl: sl.rearrange("p a b -> p (a b)").rearrange("p (kt c) -> p kt c", kt=4), src)]

    wchunk = [0]

    def wissue(j):
        s = j % NSLOT
        wchunk[0] = 0
        for dv, src in materialize(WSEQ[j]):
            dst = dv(wring[:, s])
            nk = dst.shape[1]
            step = max(1, nk // WSPLIT)
            for k0 in range(0, nk, step):
                c.dma("pool", lambda h, dst=dst, src=src, k0=k0, step=step: h.dma_start(out=dst[:, k0:k0 + step, :], in_=src[:, k0:k0 + step, :]), writes=[WB[s][wchunk[0] % 16]])
                wchunk[0] += 1

    def wslot_load(spec):
        kk = st["wk"]; st["wk"] += 1
        if wseq_in is None:
            WSEQ.append(spec)
        else:
            assert WSEQ[kk] == spec, (kk, spec, WSEQ[kk])
        while st["wi"] <= min(kk + (NSLOT - 1 if wseq_in is not None else 0), len(WSEQ) - 1):
            wissue(st["wi"]); st["wi"] += 1
        s = kk % NSLOT
        return wring[:, s], WB[s]

    def win_cols(l, cols):
        wsl, wb = wslot_load(("win", l, tuple(cols)))
        return wsl, wb, sum(n for _, n in cols)

    def win_tile(l, cols):
        wsl, wb, M = win_cols(l, cols)
        a = nxt("ps", 2); ss = nxt("pss", 32)
        P = PA[a]

        def mm(h):
            r = None
            for kt in range(16):
                h.matmul(P[0:M, 0:512], lhsT=wsl[:, kt, 0:M], rhs=hnT[:, kt, 0:512], start=(kt == 0), stop=(kt == 15))
                r = h.matmul(P[0:M, 512:1024], lhsT=wsl[:, kt, 0:M], rhs=hnT[:, kt, 512:1024], start=(kt == 0), stop=(kt == 15))
            return r
        c.op("pe", mm, reads=[wb, HB], writes=[PAB[a]])

        def mms(h):
            r = None
            for kt in range(16):
                r = h.matmul(psS[0:M, ss * 16:(ss + 1) * 16], lhsT=wsl[:, kt, 0:M], rhs=hnT[:, kt, NP:NT], start=(kt == 0), stop=(kt == 15))
            return r
        c.op("pe", mms, reads=[wb, HB], writes=[PSB[0]])
        return P[0:M, :], psS[0:M, ss * 16:(ss + 1) * 16], [PAB[a], PSB[0]]

    def ev2(eng, fn, dst, pm, psm, pb, reads=(), writes=()):
        c.op(eng, lambda h: fn(h, dst[:, 0:NP], pm, 0, NP), reads=[pb[0]] + list(reads), writes=list(writes))
        c.op(eng, lambda h: fn(h, dst[:, NP:NT], psm, NP, NT), reads=[pb[1]] + list(reads), writes=list(writes))

    def rstd_from(eng_ss_ap, n_mean, out_ap, reads, writes):
        c.op("act", lambda h: h.activation(out=out_ap, in_=eng_ss_ap, func=AF.Ln, bias=EPS, scale=1.0 / n_mean), reads=reads, writes=writes)
        c.op("act", lambda h: h.activation(out=out_ap, in_=out_ap, func=AF.Exp, scale=-0.5), reads=writes, writes=writes)

    identf = TF[0][:, 0:128]
    c.op("pool", lambda h: h.memset(identf, 0.0), writes=[CB, TFB[0]])
    c.op("pool", lambda h: h.affine_select(out=identf, in_=identf, pattern=[[-1, 128]], compare_op=ALU.not_equal, fill=1.0, base=0, channel_multiplier=1), reads=[CB], writes=[CB])
    c.op("pool", lambda h: h.memset(triU[:], 1.0), writes=[CB])
    c.op("pool", lambda h: h.affine_select(out=triU[:], in_=triU[:], pattern=[[1, 128]], compare_op=ALU.is_ge, fill=0.0, base=0, channel_multiplier=-1), reads=[CB], writes=[CB])
    c.op("dve", lambda h: h.tensor_copy(out=ident[:], in_=identf), reads=[CB, TFB[0]], writes=[CB])
    c.op("dve", lambda h: h.tensor_copy(out=triUb[:], in_=triU[:]), reads=[CB], writes=[CB])
    c.op("dve", lambda h: h.memset(onesb[:], 1.0), writes=[CB])
    c.op("dve", lambda h: h.memset(bdiag[:], 0.0), writes=[CB])
    c.op("dve", lambda h: h.memset(bdiag[0:64, 0:64], 1.0), writes=[CB])
    c.op("dve", lambda h: h.memset(bdiag[64:128, 64:128], 1.0), writes=[CB])
    for hh in range(2):
        c.op("dve", lambda h, hh=hh: h.tensor_scalar(out=swm[:, hh, 0, :], in0=triU[:], scalar1=-1.0, scalar2=1.0, op0=ALU.mult, op1=ALU.add), reads=[CB], writes=[CB])
        c.op("dve", lambda h, hh=hh: h.tensor_copy(out=swm[:, hh, 1, :], in_=triU[:]), reads=[CB], writes=[CB])
    PB0 = c.buf("params0")
    NCg = dict(allow_slow_non_contiguous=True)

    def pdma(out_ap, in_ap, buf=None, **kw):
        c.dma("sp", lambda h: h.dma_start(out=out_ap, in_=in_ap, **kw), writes=[buf or PB])

    def params_first():
        stg = TF[1]; stb = TFB[1]
        rows = [(g_norm.rearrange("l (kt p) -> (l kt) p", p=128), 0, 32, 0, 128),
                (g_mem.rearrange("l (kt p) -> (l kt) p", p=128), 32, 32, 0, 128),
                (conv_w.rearrange("l j (ct p) -> (l j ct) p", p=128), 64, 24, 0, 128),
                (g_o, 88, 2, 0, 128), (g_mq, 90, 2, 0, 128),
                (g_sq, 92, 2, 0, 64), (g_sq, 92, 2, 64, 64), (g_sk, 94, 2, 0, 64), (g_sk, 94, 2, 64, 64)]
        for src_ap, r0, nr, c0, nc_ in rows:
            c.dma("sp", lambda h, src_ap=src_ap, r0=r0, nr=nr, c0=c0, nc_=nc_: h.dma_start(out=stg[r0:r0 + nr, c0:c0 + nc_], in_=src_ap), writes=[stb])
        pdma(gmkb[:, 0, :], g_mk[0:1, :].partition_broadcast(128), PB0)
        c.op("pe", lambda h: h.matmul(psT[0][:, 0:96], lhsT=stg[0:96, 0:128], rhs=identf[0:96, 0:96], start=True, stop=True), reads=[stb, TFB[0], CB], writes=[PTB[0]])
        pv = psT[0]
        c.op("dve", lambda h: h.tensor_copy(out=gT[:].rearrange("p l k -> p (l k)"), in_=pv[:, 0:32]), reads=[PTB[0]], writes=[PB0])
        c.op("dve", lambda h: h.tensor_copy(out=gmT[:].rearrange("p l k -> p (l k)"), in_=pv[:, 32:64]), reads=[PTB[0]], writes=[PB0])
        c.op("dve", lambda h: h.tensor_copy(out=cw[:].rearrange("p l j c -> p (l j c)"), in_=pv[:, 64:88]), reads=[PTB[0]], writes=[PB0])
        c.op("dve", lambda h: h.tensor_copy(out=gov[:], in_=pv[:, 88:90]), reads=[PTB[0]], writes=[PB0])
        c.op("dve", lambda h: h.tensor_copy(out=gmq[:], in_=pv[:, 90:92]), reads=[PTB[0]], writes=[PB0])
        c.op("dve", lambda h: h.tensor_copy(out=gsq[:], in_=pv[:, 92:94]), reads=[PTB[0]], writes=[PB0])
        c.op("dve", lambda h: h.tensor_copy(out=gsk[:], in_=pv[:, 94:96]), reads=[PTB[0]], writes=[PB0])

    def load_x():
        for i in range(8):
            c.dma("sp", lambda h, i=i: h.dma_start(out=x_res[:, i, :], in_=xp[128 * i:128 * i + 128, :]), writes=[XB[i]])
        c.dma("sp", lambda h: h.dma_start(out=x_s[:], in_=xs_in), writes=[XSB])

    def params_rest():
        pdma(cm[:], cmask.partition_broadcast(128))
        for l in range(2):
            if l > 0:
                pdma(gmkb[:, l, :], g_mk[l:l + 1, :].partition_broadcast(128))
            c.dma("pool", lambda h, l=l: h.dma_start(out=wupb[0:16, l, :], in_=w_up[l]), writes=[PB])
            c.dma("pool", lambda h, l=l: h.dma_start(out=wupb[16:17, l, :], in_=b_a[l:l + 1, :]), writes=[PB])
            pdma(esk[:, l, :], sinks[l:l + 1, :].partition_broadcast(128))
        c.op("act", lambda h: h.activation(out=esk[:], in_=esk[:], func=AF.Exp), reads=[PB, PB0], writes=[PB])
        for l in range(2):
            for i4 in range(4):
                c.op("dve", lambda h, l=l, i4=i4: h.tensor_copy(out=esq[0:64, l, i4:i4 + 1], in_=esk[0:64, l, 2 * i4:2 * i4 + 1]), reads=[PB, PB0], writes=[PB])
                c.op("dve", lambda h, l=l, i4=i4: h.tensor_copy(out=esq[64:128, l, i4:i4 + 1], in_=esk[64:128, l, 2 * i4 + 1:2 * i4 + 2]), reads=[PB, PB0], writes=[PB])
        c.op("dve", lambda h: h.tensor_copy(out=swm1[:], in_=swm[:]), reads=[CB], writes=[CB])
        for hh in range(2):
            c.op("dve", lambda h, hh=hh: h.tensor_scalar(out=swm1[:, hh, 0, :], in0=swm[:, hh, 0, :], scalar1=cm[:, 8:9], scalar2=None, op0=ALU.mult), reads=[CB, PB, PB0], writes=[CB])

    c.op("dve", lambda h: h.memset(a_ext[:], 1.0), writes=[AEB])
    c.op("dve", lambda h: h.memset(pay[:], 0.0), writes=[PYB])
    params_first()
    start_hooks = {0: load_x, 1: params_rest}

    def norm_a(src_ap, m, rb, par):
        xb = xsb[par]
        ssc, ssb = smcol()
        wl = [XSBB[par]] + (MBall if xs_first[par] else [])
        xs_first[par] = False
        c.op("act", lambda h: h.activation(out=xb[0:m, :], in_=src_ap, func=AF.Square, accum_out=ssc[0:m, :]), reads=[rb], writes=wl + [ssb])
        rstd_from(ssc[0:m, :], float(D), ssc[0:m, :], [ssb], [ssb])
        c.op("dve", lambda h: h.tensor_scalar(out=xb[0:m, :], in0=src_ap, scalar1=ssc[0:m, :], scalar2=None, op0=ALU.mult), reads=[rb, ssb], writes=[XSBB[par]])

    def norm_b(m, gsrc, l, col0, par, hbufs=None):
        hbufs = hbufs or [HB]
        xb = xsb[par]
        for g4 in range(4):
            t = nxt("pt", 3)
            pv = psT[t][:].bitcast(BF16)

            def tr(h, g4=g4, pv=pv):
                r = None
                for j in range(4):
                    kt = g4 * 4 + j
                    r = h.transpose(out=pv[:, j * 128:j * 128 + m], in_=xb[0:m, kt * 128:(kt + 1) * 128], identity=ident[0:m, 0:m])
                return r
            c.op("pe", tr, reads=[XSBB[par]] + MBall + [CB], writes=[PTB[t]])
            src = pv[:, 0:512].rearrange("p (j n) -> p j n", j=4)[:, :, 0:m]
            c.op("dve", lambda h, g4=g4, src=src: h.tensor_tensor(out=hnT[:, g4 * 4:g4 * 4 + 4, col0:col0 + m], in0=src,
                 in1=gsrc[:, l, g4 * 4:g4 * 4 + 4].unsqueeze(2).to_broadcast([128, 4, m]), op=ALU.mult), reads=[PTB[t], PB, PB0], writes=hbufs)

    def norm_transpose(src_ap, m, rb, gsrc, l, col0, par, hbufs=None):
        norm_a(src_ap, m, rb, par)
        norm_b(m, gsrc, l, col0, par, hbufs)

    def mem_kv_prompt(l, inter=()):
        inter = list(inter)
        xs_first[0] = True; xs_first[1] = True
        for mt in range(2):
            c.dma("sp", lambda h, mt=mt: h.dma_start(out=memstage, in_=memp[128 * mt:128 * mt + 128, :]), writes=[WFB[0], WFB[1]])
            if l == 0:
                start_hooks[mt]()
            norm_transpose(memstage, 128, WFB[0], gmT, l, 128 * mt, mt, [HBm, HB])
            stage(1.5)
        wv = w_mkv[l].rearrange("(kt p) c -> p kt c", p=128)
        kst, kstb = WF[0], WFB[0]
        vst, vstb = WF[1], WFB[1]
        for j in range(8):
            wsl, wb = wslot_load(("mkv", l, j))
            if j == 1:
                stage(1.7)
            if j == 2:
                stage(1.75)
            if j == 4:
                stage(1.8)
            if j == 5:
                stage(1.85)
            for mt in range(2):
                t = nxt("pt", 3)

                def mm(h, mt=mt, t=t, wsl=wsl):
                    r = None
                    for kt in range(16):
                        r = h.matmul(psT[t][:, 0:128], lhsT=hnT[:, kt, 128 * mt:128 * mt + 128], rhs=wsl[:, kt, :], start=(kt == 0), stop=(kt == 15))
                    return r
                c.op("pe", mm, reads=[wb, HBm], writes=[PTB[t]])
                if j < 4:
                    hd = j
                    dst = kst[:, mt * 512 + hd * 128: mt * 512 + hd * 128 + 128]
                    ssc, ssb = smcol()
                    c.op("act", lambda h, dst=dst, t=t, ssc=ssc: h.activation(out=dst, in_=psT[t][:, 0:128], func=AF.Square, accum_out=ssc), reads=[PTB[t]], writes=[kstb, ssb])
                    rstd_from(ssc, 128.0, ssc, [ssb], [ssb])
                    c.op("dve", lambda h, dst=dst, t=t, ssc=ssc: h.scalar_tensor_tensor(out=dst, in0=psT[t][:, 0:128], scalar=ssc, in1=gmkb[:, l, :], op0=ALU.mult, op1=ALU.mult), reads=[PTB[t], ssb, PB, PB0], writes=[kstb])
                    wh = nxt("th", 2)
                    c.op("dve", lambda h, dst=dst, wh=wh: h.tensor_copy(out=TH[wh][:, 0:128], in_=dst), reads=[kstb], writes=[THB[wh]])
                    t2 = nxt("pt", 3)
                    pv = psT[t2][:].bitcast(BF16)
                    c.op("pe", lambda h, wh=wh, pv=pv: h.transpose(out=pv[:, 0:128], in_=TH[wh][:, 0:128], identity=ident[:]), reads=[THB[wh], CB], writes=[PTB[t2]])
                    c.op("act", lambda h, pv=pv, hd=hd, mt=mt: h.activation(out=memK[:, hd, 128 * mt:128 * mt + 128], in_=pv[:, 0:128], func=AF.Copy), reads=[PTB[t2]], writes=[MKB])
                else:
                    hd = j - 4
                    dst = vst[:, mt * 512 + hd * 128: mt * 512 + hd * 128 + 128]
                    c.op("act", lambda h, dst=dst, t=t: h.activation(out=dst, in_=psT[t][:, 0:128], func=AF.Copy), reads=[PTB[t]], writes=[vstb])
                    c.op("dve", lambda h, dst=dst, hd=hd, mt=mt: h.tensor_copy(out=memV[:, mt, hd * 128:hd * 128 + 128], in_=dst), reads=[vstb], writes=[MVB])
            if inter:
                inter.pop(0)()
        while inter:
            inter.pop(0)()
        stage(1.9)
        c.dma("sp", lambda h: h.dma_start(out=o_mk_p[l].rearrange("(mt p) f -> p mt f", p=128), in_=kst[:, 0:1024].rearrange("p (mt f) -> p mt f", mt=2)), reads=[kstb], writes=[OUTB])
        c.dma("sp", lambda h: h.dma_start(out=o_mv_p[l].rearrange("(mt p) f -> p mt f", p=128), in_=vst[:, 0:1024].rearrange("p (mt f) -> p mt f", mt=2)), reads=[vstb], writes=[OUTB])

    def wout_group(l, g, mbuf):
        for half in range(4):
            wsl, wb = wslot_load(("wout", l, g, half))
            wv = wsl.rearrange("p a b -> p (a b)").rearrange("p (kt c) -> p kt c", kt=4)
            for i, (c0, m) in enumerate(TT[:8] + [(NP, NST)]):
                t = nxt("pt", 3)

                def mm(h, c0=c0, m=m, t=t, wv=wv):
                    r = None
                    for kt in range(4):
                        r = h.matmul(psT[t][0:m, :], lhsT=mixT[:, kt, c0:c0 + m], rhs=wv[:, kt, :], start=(kt == 0), stop=(kt == 3))
                    return r
                c.op("pe", mm, reads=[wb] + MB[mbuf], writes=[PTB[t]])
                if i < 8:
                    dst = x_res[:, i, half * 512:(half + 1) * 512]; db = XB[i]
                else:
                    dst = x_s[:, half * 512:(half + 1) * 512]; db = XSB
                c.op("dve", lambda h, dst=dst, t=t, m=m: h.tensor_tensor(out=dst, in0=psT[t][0:m, :], in1=dst, op=ALU.add), reads=[PTB[t], db], writes=[db])

    def conv_group(l, mbuf):
        c.op("dve", lambda h: h.memset(WF[1][:, 0:2], 0.0), writes=[WFB[1]])
        for ct in range(4):
            cE, cEb = WF[0], WFB[0]; ext, extb = WF[1], WFB[1]; t1, t1b = WF[2], WFB[2]; bz, bzb = WH[1], WHB[1]
            szt, szb = WH[0], WHB[0]
            pm, psm, pb = win_tile(l, [(A_C + 128 * ct, 128)])
            ev2("act", lambda h, d, s, a, b: h.activation(out=d, in_=s, func=AF.Copy), cE, pm, psm, pb, writes=[cEb])
            pm, psm, pb = win_tile(l, [(A_H + 128 * ct, 128)])
            ev2("dve", lambda h, d, s, a, b: h.tensor_tensor(out=d, in0=s, in1=cE[:, a:b], op=ALU.mult), ext[:, 2:2 + NT], pm, psm, pb, reads=[cEb], writes=[extb])
            c.op("dve", lambda h, ct=ct: h.tensor_scalar(out=t1[:, 0:NT], in0=ext[:, 0:NT], scalar1=cw[:, l, 0, ct:ct + 1], scalar2=None, op0=ALU.mult), reads=[extb, PB, PB0], writes=[t1b])
            c.op("dve", lambda h, ct=ct: h.scalar_tensor_tensor(out=t1[:, 0:NT], in0=ext[:, 1:1 + NT], scalar=cw[:, l, 1, ct:ct + 1], in1=t1[:, 0:NT], op0=ALU.mult, op1=ALU.add), reads=[extb, PB, t1b], writes=[t1b])
            c.op("dve", lambda h, ct=ct: h.scalar_tensor_tensor(out=t1[:, 0:NT], in0=ext[:, 2:2 + NT], scalar=cw[:, l, 2, ct:ct + 1], in1=t1[:, 0:NT], op0=ALU.mult, op1=ALU.add), reads=[extb, PB, t1b], writes=[t1b])
            pm, psm, pb = win_tile(l, [(A_Z + 128 * ct, 128)])
            ev2("act", lambda h, d, s, a, b: h.activation(out=d, in_=s, func=AF.Silu), szt, pm, psm, pb, writes=[szb])
            pm, psm, pb = win_tile(l, [(A_B + 128 * ct, 128)])
            ev2("dve", lambda h, d, s, a, b: h.tensor_tensor(out=d, in0=s, in1=szt[:, a:b], op=ALU.mult), bz, pm, psm, pb, reads=[szb], writes=[bzb])
            c.op("dve", lambda h, ct=ct: h.tensor_tensor(out=mixT[:, ct, :], in0=bz[:, 0:NT], in1=t1[:, 0:NT], op=ALU.mult), reads=[bzb, t1b], writes=[MB[mbuf][ct]])
            for si, (s0, n) in enumerate(SEGS):
                c.op("act", lambda h, ct=ct, si=si, s0=s0: h.copy(out=fix_t1[:, ct, si, :], in_=t1[:, s0:s0 + 2]), reads=[t1b], writes=[FXB])
                c.op("act", lambda h, ct=ct, si=si, s0=s0: h.copy(out=fix_bz[:, ct, si, :], in_=bz[:, s0:s0 + 2]), reads=[bzb], writes=[FXB])
                c.op("act", lambda h, ct=ct, si=si, s0=s0: h.copy(out=fix_lk[:, ct, si, :], in_=ext[:, s0:s0 + 2]), reads=[extb], writes=[FXB])
                c.op("act", lambda h, ct=ct, si=si, s0=s0, n=n: h.copy(out=ulast[:, ct, si, :], in_=ext[:, s0 + n:s0 + n + 2]), reads=[extb], writes=[FXB])

    def conv_fix(l, mbuf):
        w0 = cw[:, l, 0, :].unsqueeze(2).to_broadcast([128, 4, 5])
        w1 = cw[:, l, 1, :].unsqueeze(2).to_broadcast([128, 4, 5])
        c.op("dve", lambda h: h.tensor_tensor(out=cprev[:], in0=cprev[:], in1=fix_lk[:], op=ALU.subtract), reads=[CPB, FXB], writes=[CPB, PB0])
        c.op("dve", lambda h: h.tensor_tensor(out=fix_lk[:, :, :, 0], in0=cprev[:, :, :, 0], in1=w0, op=ALU.mult), reads=[CPB, PB, PB0], writes=[FXB])
        c.op("dve", lambda h: h.tensor_tensor(out=fix_lk[:, :, :, 1], in0=cprev[:, :, :, 1], in1=w1, op=ALU.mult), reads=[CPB, PB, FXB], writes=[FXB])
        c.op("dve", lambda h: h.tensor_tensor(out=fix_t1[:, :, :, 0], in0=fix_lk[:, :, :, 0], in1=fix_lk[:, :, :, 1], op=ALU.add), reads=[FXB], writes=[FXB])
        c.op("dve", lambda h: h.tensor_tensor(out=fix_t1[:, :, :, 1], in0=cprev[:, :, :, 1], in1=w0, op=ALU.mult), reads=[CPB, PB, FXB], writes=[FXB])
        c.op("dve", lambda h: h.tensor_tensor(out=fix_t1[:], in0=fix_t1[:], in1=fix_bz[:], op=ALU.mult), reads=[FXB], writes=[FXB])
        dS = qn_s; dP = sz_s
        c.op("dve", lambda h: h.memset(dS[:], 0.0), writes=[QSB])
        c.op("dve", lambda h: h.tensor_copy(out=dP[:, :, 0:2], in_=fix_t1[:, :, 0, :]), reads=[FXB], writes=[QSB])
        for q in range(NSQ):
            c.op("dve", lambda h, q=q: h.tensor_copy(out=dS[:, :, 4 * q:4 * q + 2], in_=fix_t1[:, :, 1 + q, :]), reads=[FXB], writes=[QSB])
        for half in range(4):
            wsl, wb = wslot_load(("wout", l, 0, half))
            wv = wsl.rearrange("p a b -> p (a b)").rearrange("p (kt c) -> p kt c", kt=4)
            t = nxt("pt", 3)

            def mmp(h, t=t, wv=wv):
                r = None
                for kt in range(4):
                    r = h.matmul(psT[t][0:2, :], lhsT=dP[:, kt, 0:2], rhs=wv[:, kt, :], start=(kt == 0), stop=(kt == 3))
                return r
            c.op("pe", mmp, reads=[wb, QSB], writes=[PTB[t]])
            dst = x_res[0:2, 0, half * 512:(half + 1) * 512]
            c.op("dve", lambda h, dst=dst, t=t: h.tensor_tensor(out=dst, in0=psT[t][0:2, :], in1=dst, op=ALU.add), reads=[PTB[t], XB[0]], writes=[XB[0]])
            t = nxt("pt", 3)

            def mms_(h, t=t, wv=wv):
                r = None
                for kt in range(4):
                    r = h.matmul(psT[t][0:NST, :], lhsT=dS[:, kt, :], rhs=wv[:, kt, :], start=(kt == 0), stop=(kt == 3))
                return r
            c.op("pe", mms_, reads=[wb, QSB], writes=[PTB[t]])
            dst2 = x_s[:, half * 512:(half + 1) * 512]
            c.op("dve", lambda h, dst2=dst2, t=t: h.tensor_tensor(out=dst2, in0=psT[t][0:NST, :], in1=dst2, op=ALU.add), reads=[PTB[t], XSB], writes=[XSB])
        for j in range(2):
            c.dma("sp", lambda h, j=j: h.dma_start(out=o_conv_p[l, j:j + 1, :].rearrange("o (ct p) -> p (o ct)", p=128), in_=ulast[:, :, 0, j], **NCg), reads=[FXB], writes=[OUTB])
            for q in range(NSQ):
                c.dma("sp", lambda h, j=j, q=q: h.dma_start(out=o_conv_s[l, q, j:j + 1, :].rearrange("o (ct p) -> p (o ct)", p=128), in_=ulast[:, :, 1 + q, j], **NCg), reads=[FXB], writes=[OUTB])

    def tm_pass(l):
        wv = w_in[l].rearrange("(kt p) c -> p kt c", p=128)
        for j in range(5):
            c0 = G_V + 128 * j if j < 4 else S_V
            wsl, wb = wslot_load(("win", l, ((c0, 128),)))
            for i, (t0, m) in enumerate(TT):
                t = nxt("pt", 3)

                def mm(h, t0=t0, m=m, t=t, wsl=wsl):
                    r = None
                    for kt in range(16):
                        r = h.matmul(psT[t][0:m, 0:128], lhsT=hnT[:, kt, t0:t0 + m], rhs=wsl[:, kt, :], start=(kt == 0), stop=(kt == 15))
                    return r
                c.op("pe", mm, reads=[wb, HB], writes=[PTB[t]])
                if j < 4:
                    c.op("act", lambda h, i=i, j=j, m=m, t=t: h.activation(out=v_tm[0:m, i, 128 * j:128 * j + 128], in_=psT[t][0:m, 0:128], func=AF.Copy), reads=[PTB[t]], writes=[VB[i]])
                else:
                    c.op("act", lambda h, i=i, m=m, t=t: h.activation(out=vs_tm[0:m, i, :], in_=psT[t][0:m, 0:128], func=AF.Copy), reads=[PTB[t]], writes=[VSB[i]])
                    if i == 7:
                        c.op("act", lambda h, t=t: h.activation(out=pay[:, 386:514], in_=psT[t][:, 0:128], func=AF.Copy), reads=[PTB[t]], writes=[PYB])
                    if i >= 8:
                        q = i - 8
                        wf = nxt("tf", 3)
                        c.op("act", lambda h, t=t, wf=wf: h.activation(out=TF[wf][0:4, 0:128], in_=psT[t][0:4, 0:128], func=AF.Copy), reads=[PTB[t]], writes=[TFB[wf]])
                        c.dma("sp", lambda h, q=q, wf=wf: h.dma_start(out=o_swv_s[l, q, 124:128, :], in_=TF[wf][0:4, 0:128]), reads=[TFB[wf]], writes=[OUTB])
        pm, psm, pb = win_tile(l, [(G_A, 16)])
        ev2("act", lambda h, d, s, a, b: h.activation(out=d, in_=s, func=AF.Copy), a_ext[0:16, :], pm, psm, pb, writes=[AEB])
        for p in range(2):
            pm, psm, pb = win_tile(l, [(G_K + 128 * p, 128)])
            ev2("act", lambda h, d, s, a, b: h.activation(out=d, in_=s, func=AF.Copy), kT_bf[:, p, :], pm, psm, pb, writes=[KTB[p]])
        for kv in range(2):
            pm, psm, pb = win_tile(l, [(S_K + 64 * kv, 64), (S_K + 64 * kv, 64)])
            qf, qfb = WF[0], WFB[0]; sq, sqb = WH[0], WHB[0]; rs, rsb = WF[1], WFB[1]
            ev2("act", lambda h, d, s, a, b: h.activation(out=d, in_=s, func=AF.Copy), qf, pm, psm, pb, writes=[qfb])
            headnorm(qf, qfb, sq, sqb, rs, rsb, bdiag, 64.0)
            c.op("dve", lambda h, kv=kv: h.scalar_tensor_tensor(out=kn[:, kv, :], in0=qf[:, 0:NT], scalar=gsk[:, l:l + 1], in1=rs[:, 0:NT], op0=ALU.mult, op1=ALU.mult), reads=[qfb, rsb, PB, PB0], writes=[KNB[kv]])
            for i in [7] + list(range(8, 12)):
                t0, m = TT[i]
                t = nxt("pt", 3)
                pv = psT[t][:].bitcast(BF16)
                c.op("pe", lambda h, kv=kv, t0=t0, m=m, pv=pv: h.transpose(out=pv[0:m, 0:128], in_=kn[:, kv, t0:t0 + m], identity=ident[:]), reads=[KNB[kv], CB], writes=[PTB[t]])
                if i == 7:
                    c.op("dve", lambda h, kv=kv, pv=pv: h.tensor_copy(out=pay[:, 258 + 64 * kv:258 + 64 * kv + 64], in_=pv[:, 0:64]), reads=[PTB[t]], writes=[PYB])
                else:
                    q = i - 8
                    wf = nxt("tf", 3)
                    c.op("dve", lambda h, pv=pv, wf=wf: h.tensor_copy(out=TF[wf][0:4, 0:64], in_=pv[0:4, 0:64]), reads=[PTB[t]], writes=[TFB[wf]])
                    c.dma("sp", lambda h, q=q, kv=kv, wf=wf: h.dma_start(out=o_swk_s[l, q, 124:128, 64 * kv:64 * kv + 64], in_=TF[wf][0:4, 0:64]), reads=[TFB[wf]], writes=[OUTB])
        c.dma("sp", lambda h: h.dma_start(out=o_swk_p[l], in_=pay[:, 258:386]), reads=[PYB], writes=[OUTB])
        c.dma("sp", lambda h: h.dma_start(out=o_swv_p[l], in_=pay[:, 386:514]), reads=[PYB], writes=[OUTB])
        for q in range(NSQ):
            c.dma("sp", lambda h, q=q: h.dma_start(out=o_swk_s[l, q, 0:124, :], in_=c_swk[l, q, 4:128, :]), writes=[OUTB])
            c.dma("sp", lambda h, q=q: h.dma_start(out=o_swv_s[l, q, 0:124, :], in_=c_swv[l, q, 4:128, :]), writes=[OUTB])

    def headnorm(qf, qfb, sq, sqb, rs, rsb, onesm, hd):
        c.op("act", lambda h: h.activation(out=sq[:, 0:NT], in_=qf[:, 0:NT], func=AF.Square), reads=[qfb], writes=[sqb])
        for (a, b) in ((0, 512), (512, 1024), (1024, NT)):
            t = nxt("pt", 3)
            c.op("pe", lambda h, a=a, b=b, t=t: h.matmul(psT[t][:, 0:b - a], lhsT=onesm[:], rhs=sq[:, a:b], start=True, stop=True), reads=[sqb, CB], writes=[PTB[t]])
            c.op("act", lambda h, a=a, b=b, t=t: h.activation(out=rs[:, a:b], in_=psT[t][:, 0:b - a], func=AF.Ln, bias=EPS, scale=1.0 / hd), reads=[PTB[t]], writes=[rsb])
        c.op("act", lambda h: h.activation(out=rs[:, 0:NT], in_=rs[:, 0:NT], func=AF.Exp, scale=-0.5), reads=[rsb], writes=[rsb])

    def sp_tile(l, i):
        t0, m = TT[i]
        t = nxt("pt", 3)
        c.op("pe", lambda h: h.matmul(psT[t][0:m, 0:256], lhsT=a_ext[:, t0:t0 + m], rhs=wupb[:, l, :], start=True, stop=True), reads=[AEB, PB, PB0], writes=[PTB[t]])
        wf = nxt("tf", 3)
        sp = TF[wf]
        c.op("act", lambda h: h.activation(out=sp[0:m, 0:256], in_=psT[t][0:m, 0:256], func=AF.Exp, scale=-1.0), reads=[PTB[t]], writes=[TFB[wf]])
        c.op("act", lambda h: h.activation(out=sp[0:m, 0:256], in_=sp[0:m, 0:256], func=AF.Ln, bias=1.0), reads=[TFB[wf]], writes=[TFB[wf]])
        return sp, TFB[wf]

    def gla_chain(l):
        for p in range(2):
            c.op("dve", lambda h, p=p: h.memset(S_run[:, p, :], 0.0), writes=[SRB[p]])
        c.op("dve", lambda h: h.memset(eG[:], 1.0), writes=[EGB])
        for i, (t0, m) in enumerate(TT):
            sp, spb = sp_tile(l, i)
            for p in range(2):
                t = nxt("pt", 3)
                c.op("pe", lambda h, t=t, p=p: h.matmul(psT[t][:, 0:m], lhsT=sp[0:m, 128 * p:128 * p + 128], rhs=triU[0:m, 0:m], start=True, stop=True), reads=[spb, CB], writes=[PTB[t]])
                nb, nbb = smcol()
                c.op("dve", lambda h, t=t, nb=nb: h.tensor_scalar(out=nb, in0=psT[t][:, m - 1:m], scalar1=-1.0 / 16, scalar2=None, op0=ALU.mult), reads=[PTB[t]], writes=[nbb])
                wf = nxt("tf", 3)
                et = TF[wf]
                c.op("act", lambda h, t=t, nb=nb, et=et: h.activation(out=et[:, 0:m], in_=psT[t][:, 0:m], func=AF.Exp, scale=1.0 / 16, bias=nb), reads=[PTB[t], nbb], writes=[TFB[wf]])
                dt_, dtb = smcol()
                c.op("act", lambda h, nb=nb, dt_=dt_: h.activation(out=dt_, in_=nb, func=AF.Exp), reads=[nbb], writes=[dtb])
                wh = nxt("th", 2)
                c.op("dve", lambda h, et=et, wh=wh, p=p: h.tensor_tensor(out=TH[wh][:, 0:m], in0=kT_bf[:, p, t0:t0 + m], in1=et[:, 0:m], op=ALU.mult), reads=[KTB[p], TFB[wf]], writes=[THB[wh]])
                t2 = nxt("pt", 3)
                pv = psT[t2][:].bitcast(BF16)
                c.op("pe", lambda h, wh=wh, pv=pv: h.transpose(out=pv[0:m, 0:128], in_=TH[wh][:, 0:m], identity=ident[:]), reads=[THB[wh], CB], writes=[PTB[t2]])
                wh2 = wh
                c.op("act", lambda h, wh2=wh2, pv=pv: h.activation(out=TH[wh2][0:m, 512:640], in_=pv[0:m, 0:128], func=AF.Copy), reads=[PTB[t2]], writes=[THB[wh2]])
                t3 = nxt("pt", 3)

                def mm(h, t3=t3, wh2=wh2, p=p):
                    r = None
                    for hh in range(2):
                        hd = 2 * p + hh
                        r = h.matmul(psT[t3][64 * hh:64 * hh + 64, 0:128], lhsT=TH[wh2][0:m, 512 + 64 * hh:512 + 64 * hh + 64], rhs=v_tm[0:m, i, 128 * hd:128 * hd + 128], start=True, stop=True)
                    return r
                c.op("pe", mm, reads=[THB[wh2], VB[i]], writes=[PTB[t3]])
                if i < 8:
                    c.op("act", lambda h, p=p: h.copy(out=Sloc[:, p, i, :], in_=S_run[:, p, :]), reads=[SRB[p]], writes=[SLB[p][i]])
                    c.op("dve", lambda h, p=p, t3=t3, dt_=dt_: h.scalar_tensor_tensor(out=S_run[:, p, :], in0=S_run[:, p, :], scalar=dt_, in1=psT[t3][:, 0:128], op0=ALU.mult, op1=ALU.add), reads=[SRB[p], dtb, PTB[t3]], writes=[SRB[p]])
                    c.op("dve", lambda h, p=p, dt_=dt_: h.tensor_tensor(out=eG[:, p, i + 1:i + 2], in0=eG[:, p, i:i + 1], in1=dt_, op=ALU.mult), reads=[EGB, dtb], writes=[EGB])
                else:
                    q = i - 8
                    wf2 = nxt("tf", 3)
                    c.op("dve", lambda h, p=p, q=q, t3=t3, dt_=dt_, wf2=wf2: h.scalar_tensor_tensor(out=TF[wf2][:, 0:128], in0=S_st[:, p, 1 + q, :], scalar=dt_, in1=psT[t3][:, 0:128], op0=ALU.mult, op1=ALU.add), reads=[SSB, dtb, PTB[t3]], writes=[TFB[wf2]])
                    c.dma("sp", lambda h, p=p, q=q, wf2=wf2: h.dma_start(out=o_gla_s[l, q, 128 * p:128 * p + 128, :], in_=TF[wf2][:, 0:128]), reads=[TFB[wf2]], writes=[OUTB])
        for p in range(2):
            c.op("dve", lambda h, p=p: h.tensor_copy(out=pay[:, 128 * p:128 * p + 128], in_=S_run[:, p, :]), reads=[SRB[p]], writes=[PYB])
            c.op("dve", lambda h, p=p: h.tensor_copy(out=pay[:, 256 + p:257 + p], in_=eG[:, p, 8:9]), reads=[EGB], writes=[PYB])

    def exchange(l):
        c.op("dve", lambda h: h.tensor_copy(out=pay[:, 514:522].rearrange("p (a b) -> p a b", a=4), in_=ulast[:, :, 0, :]), reads=[FXB], writes=[PYB])
        c.dma("sp", lambda h: h.dma_start(out=agin[l].ap(), in_=pay[:]), reads=[PYB], writes=[AGB[l][0]])
        c.op("pool", lambda h: h.collective_compute("AllGather", ALU.bypass, replica_groups=[[0, 1, 2, 3], [4, 5, 6, 7]],
             ins=[agin[l].ap().opt()], outs=[agout[l].ap().opt()]), reads=[AGB[l][0]], writes=[AGB[l][1]])

    def kprev_make(src_f32_ap, srcb, seg):
        for kv in range(2):
            wh = nxt("th", 2)
            for hh in range(2):
                c.op("dve", lambda h, wh=wh, kv=kv, hh=hh: h.tensor_copy(out=TH[wh][:, 64 * hh:64 * hh + 64], in_=src_f32_ap[:, 64 * kv:64 * kv + 64]), reads=[srcb], writes=[THB[wh]])
            t = nxt("pt", 3)
            pv = psT[t][:].bitcast(BF16)
            c.op("pe", lambda h, wh=wh, pv=pv: h.transpose(out=pv[:, 0:128], in_=TH[wh][:, 0:128], identity=ident[:]), reads=[THB[wh], CB], writes=[PTB[t]])
            c.op("act", lambda h, kv=kv, pv=pv: h.activation(out=kprev[:, kv, seg, :], in_=pv[:, 0:128], func=AF.Copy), reads=[PTB[t]], writes=[KPB[seg]])

    def combine(l):
        for p in range(2):
            c.op("dve", lambda h, p=p: h.memset(S_st[:, p, 0, :], 0.0), writes=[SSB])
        acc, accb = WF[2], WFB[2]
        c.op("dve", lambda h: h.memset(acc[:, 0:272], 0.0), writes=[accb])
        for r in range(4):
            wf = nxt("tf", 3)
            g, gb = TF[wf], TFB[wf]
            c.dma("sp", lambda h, r=r, g=g: h.dma_start(out=g[:, 0:PAYC], in_=agout[l].ap()[128 * r:128 * r + 128, :]), reads=[AGB[l][1]], writes=[gb])
            dcol, dcb = smcol(); dcol2, dcb2 = smcol()
            for p, dc, db in ((0, dcol, dcb), (1, dcol2, dcb2)):
                c.op("dve", lambda h, p=p, dc=dc, g=g: h.tensor_scalar(out=dc, in0=g[:, 256 + p:257 + p], scalar1=-1.0, scalar2=cm[:, r:r + 1], op0=ALU.add, op1=ALU.mult), reads=[gb, PB, PB0], writes=[db])
                c.op("dve", lambda h, dc=dc: h.tensor_scalar(out=dc, in0=dc, scalar1=1.0, scalar2=None, op0=ALU.add), reads=[db], writes=[db])
                c.op("dve", lambda h, p=p, g=g: h.tensor_scalar(out=g[:, 128 * p:128 * p + 128], in0=g[:, 128 * p:128 * p + 128], scalar1=cm[:, r:r + 1], scalar2=None, op0=ALU.mult), reads=[gb, PB, PB0], writes=[gb])
                c.op("dve", lambda h, p=p, dc=dc, g=g: h.scalar_tensor_tensor(out=S_st[:, p, 0, :], in0=S_st[:, p, 0, :], scalar=dc, in1=g[:, 128 * p:128 * p + 128], op0=ALU.mult, op1=ALU.add), reads=[SSB, db, gb], writes=[SSB])
            c.op("dve", lambda h, g=g: h.scalar_tensor_tensor(out=acc[:, 0:264], in0=g[:, 258:522], scalar=cm[:, 4 + r:5 + r], in1=acc[:, 0:264], op0=ALU.mult, op1=ALU.add), reads=[gb, PB, accb], writes=[accb])
        kprev_make(acc[:, 0:128], accb, 0)
        c.op("dve", lambda h: h.tensor_copy(out=vprev[:, 0, :], in_=acc[:, 128:256]), reads=[accb], writes=[VPB[0]])
        c.op("dve", lambda h: h.tensor_copy(out=cprev[:, :, 0, :], in_=acc[:, 256:264].rearrange("p (a b) -> p a b", a=4)), reads=[accb], writes=[CPB, PB0])

    def sample_states(l):
        stg, stgb = WF[2], WFB[2]
        c.dma("sp", lambda h: h.dma_start(out=stg[:, 0:512].rearrange("p (q d) -> p q d", q=4), in_=c_swk[l].rearrange("q s d -> s q d")), writes=[stgb])
        c.dma("sp", lambda h: h.dma_start(out=stg[:, 512:1024].rearrange("p (q d) -> p q d", q=4), in_=c_swv[l].rearrange("q s d -> s q d")), writes=[stgb])
        for q in range(NSQ):
            for j in range(2):
                c.dma("sp", lambda h, q=q, j=j: h.dma_start(out=cprev[:, :, 1 + q, j], in_=st_conv[l, q, j:j + 1, :].rearrange("o (ct p) -> p (o ct)", p=128), **NCg), writes=[CPB, PB0])
            c.dma("sp", lambda h, q=q: h.dma_start(out=S_st[:, :, 1 + q, :], in_=st_gla[l, q].rearrange("(pr hh) d v -> (hh d) pr v", pr=2)), writes=[SSB])
        for q in range(NSQ):
            kprev_make(stg[:, 128 * q:128 * q + 128], stgb, 1 + q)
            c.op("dve", lambda h, q=q: h.tensor_copy(out=vprev[:, 1 + q, :], in_=stg[:, 512 + 128 * q:512 + 128 * q + 128]), reads=[stgb], writes=[VPB[1 + q]])

    def mem_group(l, mbuf):
        isq = 1.0 / np.sqrt(128.0)

        def attend(hd, qn_ap, n, c0, dst_fn, qb):
            def sc(h):
                h.matmul(psA[:, 0:n], lhsT=memK[:, hd, 0:128], rhs=qn_ap, start=True, stop=True)
                return h.matmul(psA[:, 512:512 + n], lhsT=memK[:, hd, 128:256], rhs=qn_ap, start=True, stop=True)
            c.op("pe", sc, reads=[MKB] + qb, writes=[PAB[0]])
            wh = nxt("th", 2)
            pb_ = TH[wh]
            for mt in range(2):
                c.op("act", lambda h, mt=mt, pb_=pb_: h.activation(out=pb_[:, 512 * mt:512 * mt + n], in_=psA[:, 512 * mt:512 * mt + n], func=AF.Exp, scale=isq), reads=[PAB[0]], writes=[THB[wh]])

            def pv(h):
                for mt in range(2):
                    h.matmul(psB[:, 0:n], lhsT=memV[:, mt, 128 * hd:128 * hd + 128], rhs=pb_[:, 512 * mt:512 * mt + n], start=(mt == 0), stop=(mt == 1))
                r = None
                for mt in range(2):
                    r = h.matmul(psB[:, 512:512 + n], lhsT=onesb[:], rhs=pb_[:, 512 * mt:512 * mt + n], start=(mt == 0), stop=(mt == 1))
                return r
            c.op("pe", pv, reads=[MVB, THB[wh], CB], writes=[PAB[1]])
            wf = nxt("tf", 3)
            c.op("dve", lambda h, wf=wf: h.reciprocal(out=TF[wf][:, 0:n], in_=psB[:, 512:512 + n]), reads=[PAB[1]], writes=[TFB[wf]])
            dst_fn(wf)

        for hd in range(4):
            pm, psm, pb = win_tile(l, [(M_Q + 128 * hd, 128)])
            qf, qfb = WF[0], WFB[0]; sq, sqb = WH[0], WHB[0]; rs, rsb = WF[1], WFB[1]
            qn_, qnb = WH[1], WHB[1]; szt, szb = WF[2], WFB[2]
            ev2("act", lambda h, d, s, a, b: h.activation(out=d, in_=s, func=AF.Copy), qf, pm, psm, pb, writes=[qfb])
            headnorm(qf, qfb, sq, sqb, rs, rsb, onesb, 128.0)
            c.op("dve", lambda h: h.scalar_tensor_tensor(out=qn_[:, 0:NT], in0=qf[:, 0:NT], scalar=gmq[:, l:l + 1], in1=rs[:, 0:NT], op0=ALU.mult, op1=ALU.mult), reads=[qfb, rsb, PB, PB0], writes=[qnb])
            pm, psm, pb = win_tile(l, [(M_Z + 128 * hd, 128)])
            ev2("act", lambda h, d, s, a, b: h.activation(out=d, in_=s, func=AF.Silu), szt, pm, psm, pb, writes=[szb])
            c.op("act", lambda h, hd=hd: h.copy(out=qn_s[:, hd, :], in_=qn_[:, NP:NT]), reads=[qnb], writes=[QSB])
            c.op("act", lambda h, hd=hd: h.copy(out=sz_s[:, hd, :], in_=szt[:, NP:NT]), reads=[szb], writes=[QSB])
            for half in range(2):
                c0 = 512 * half

                def dst(wf, c0=c0, hd=hd):
                    c.op("dve", lambda h: h.tensor_tensor(out=TF[wf][:, 0:512], in0=psB[:, 0:512], in1=TF[wf][:, 0:512], op=ALU.mult), reads=[PAB[1], TFB[wf]], writes=[TFB[wf]])
                    c.op("dve", lambda h: h.tensor_tensor(out=mixT[:, hd, c0:c0 + 512], in0=TF[wf][:, 0:512], in1=szt[:, c0:c0 + 512], op=ALU.mult), reads=[TFB[wf], szb], writes=[MB[mbuf][hd]])
                attend(hd, qn_[:, c0:c0 + 512], 512, c0, dst, [qnb])
        for q in range(NSQ):
            kst, kstb = WF[0], WFB[0]
            c.dma("sp", lambda h, q=q: h.dma_start(out=kst[:, 0:1024].rearrange("p (mt f) -> p mt f", mt=2), in_=c_mk[l, q].rearrange("(mt p) f -> p mt f", p=128)), writes=[kstb])
            for mt in range(2):
                for hd in range(4):
                    wh = nxt("th", 2)
                    c.op("dve", lambda h, wh=wh, mt=mt, hd=hd: h.tensor_copy(out=TH[wh][:, 0:128], in_=kst[:, 512 * mt + 128 * hd:512 * mt + 128 * hd + 128]), reads=[kstb], writes=[THB[wh]])
                    t = nxt("pt", 3)
                    pv = psT[t][:].bitcast(BF16)
                    c.op("pe", lambda h, wh=wh, pv=pv: h.transpose(out=pv[:, 0:128], in_=TH[wh][:, 0:128], identity=ident[:]), reads=[THB[wh], CB], writes=[PTB[t]])
                    c.op("act", lambda h, pv=pv, hd=hd, mt=mt: h.activation(out=memK[:, hd, 128 * mt:128 * mt + 128], in_=pv[:, 0:128], func=AF.Copy), reads=[PTB[t]], writes=[MKB])
            vst, vstb = WF[1], WFB[1]
            c.dma("sp", lambda h, q=q: h.dma_start(out=vst[:, 0:1024].rearrange("p (mt f) -> p mt f", mt=2), in_=c_mv[l, q].rearrange("(mt p) f -> p mt f", p=128)), writes=[vstb])
            c.op("dve", lambda h: h.tensor_copy(out=WH[2][:, 0:1024], in_=vst[:, 0:1024]), reads=[vstb], writes=[MVB])
            for hd in range(4):
                def dst(wf, hd=hd, q=q):
                    c.op("dve", lambda h: h.tensor_tensor(out=TF[wf][:, 0:4], in0=psB[:, 0:4], in1=TF[wf][:, 0:4], op=ALU.mult), reads=[PAB[1], TFB[wf]], writes=[TFB[wf]])
                    c.op("dve", lambda h: h.tensor_tensor(out=mixT[:, hd, NP + 4 * q:NP + 4 * q + 4], in0=TF[wf][:, 0:4], in1=sz_s[:, hd, 4 * q:4 * q + 4], op=ALU.mult), reads=[TFB[wf], QSB], writes=[MB[mbuf][hd]])
                attend(hd, qn_s[:, hd, 4 * q:4 * q + 4], 4, 0, dst, [QSB])

    def swa_group(l, mbuf):
        for i4 in range(4):
            kv = i4 // 2
            pm, psm, pb = win_tile(l, [(S_Q + 128 * i4, 128)])
            qf, qfb = WF[0], WFB[0]; sq, sqb = WH[0], WHB[0]; rs, rsb = WF[1], WFB[1]
            qn_, qnb = WH[1], WHB[1]; szt, szb = WH[2], WHB[2]; osw, oswb = WF[2], WFB[2]
            ev2("act", lambda h, d, s, a, b: h.activation(out=d, in_=s, func=AF.Copy), qf, pm, psm, pb, writes=[qfb])
            headnorm(qf, qfb, sq, sqb, rs, rsb, bdiag, 64.0)
            qn1, qn1b = WH[3], WHB[3]
            c.op("dve", lambda h: h.scalar_tensor_tensor(out=qn_[0:64, 0:NT], in0=qf[0:64, 0:NT], scalar=gsq[0:64, l:l + 1], in1=rs[0:64, 0:NT], op0=ALU.mult, op1=ALU.mult), reads=[qfb, rsb, PB, PB0], writes=[qnb])
            c.op("dve", lambda h: h.memset(qn_[64:128, 0:NT], 0.0), writes=[qnb])
            c.op("dve", lambda h: h.scalar_tensor_tensor(out=qn1[64:128, 0:NT], in0=qf[64:128, 0:NT], scalar=gsq[64:128, l:l + 1], in1=rs[64:128, 0:NT], op0=ALU.mult, op1=ALU.mult), reads=[qfb, rsb, PB, PB0], writes=[qn1b])
            c.op("dve", lambda h: h.memset(qn1[0:64, 0:NT], 0.0), writes=[qn1b])
            qh = [qn_, qn1]
            pm, psm, pb = win_tile(l, [(S_Z + 128 * i4, 128)])
            ev2("act", lambda h, d, s, a, b: h.activation(out=d, in_=s, func=AF.Silu), szt, pm, psm, pb, writes=[szb])
            def stage1(i):
                t0, m = TT[i]
                if i == 0 or i >= 8:
                    seg = 0 if i == 0 else i - 7
                    kp = kprev[:, kv, seg, :]; kpb = KPB[seg]
                    vp = vprev[:, seg, 64 * kv:64 * kv + 64]; vpb = VPB[seg]
                else:
                    kp = kn[:, kv, t0 - 128:t0]; kpb = KNB[kv]
                    vp = vs_tm[:, i - 1, 64 * kv:64 * kv + 64]; vpb = VSB[i - 1]
                msk = swm1 if i == 0 else swm
                t = nxt("pt", 3)
                scv = psT[t][:].rearrange("p (hh b n) -> p hh b n", hh=2, b=2)

                def sc(h, scv=scv, kp=kp, t0=t0, m=m, qh=qh):
                    r = None
                    for hh in range(2):
                        lo = 64 * hh
                        h.matmul(scv[:, hh, 0, 0:m], lhsT=kp, rhs=qh[hh][:, t0:t0 + m], start=True, stop=True)
                        r = h.matmul(scv[0:m, hh, 1, 0:m], lhsT=kn[:, kv, t0:t0 + m], rhs=qh[hh][:, t0:t0 + m], start=True, stop=True)
                    return r
                c.op("pe", sc, reads=[kpb, KNB[kv], qnb, qn1b], writes=[PTB[t]])
                wh = nxt("th", 2)
                pbv = TH[wh][:, 0:512].rearrange("p (hh b n) -> p hh b n", hh=2, b=2)
                WHB_wh = THB[wh]
                if m == 128:
                    c.op("act", lambda h, t=t, wh=wh: h.activation(out=TH[wh][:, 0:512], in_=psT[t][:, 0:512], func=AF.Exp, scale=0.125), reads=[PTB[t]], writes=[WHB_wh])
                    c.op(SWA_MASK_ENG, lambda h, wh=wh, msk=msk: h.tensor_tensor(out=TH[wh][:, 0:512], in0=TH[wh][:, 0:512], in1=msk[:].rearrange("p a b n -> p (a b n)"), op=ALU.mult), reads=[WHB_wh, CB], writes=[WHB_wh])
                else:
                    c.op("act", lambda h, scv=scv, pbv=pbv, m=m: h.activation(out=pbv[:, :, 0, 0:m], in_=scv[:, :, 0, 0:m], func=AF.Exp, scale=0.125), reads=[PTB[t]], writes=[WHB_wh])
                    c.op("act", lambda h, scv=scv, pbv=pbv, m=m: h.activation(out=pbv[0:m, :, 1, 0:m], in_=scv[0:m, :, 1, 0:m], func=AF.Exp, scale=0.125), reads=[PTB[t]], writes=[WHB_wh])
                    c.op(SWA_MASK_ENG, lambda h, pbv=pbv, msk=msk, m=m: h.tensor_tensor(out=pbv[:, :, 0, 0:m], in0=pbv[:, :, 0, 0:m], in1=msk[:, :, 0, 0:m], op=ALU.mult), reads=[WHB_wh, CB], writes=[WHB_wh])
                    c.op(SWA_MASK_ENG, lambda h, pbv=pbv, msk=msk, m=m: h.tensor_tensor(out=pbv[0:m, :, 1, 0:m], in0=pbv[0:m, :, 1, 0:m], in1=msk[0:m, :, 1, 0:m], op=ALU.mult), reads=[WHB_wh, CB], writes=[WHB_wh])
                return (t0, m, pbv, WHB_wh, vp, vpb)

            def stage2(i, carry):
                t0, m, pbv, WHB_wh, vp, vpb = carry
                t2 = nxt("pt", 3)

                def pvm(h, t2=t2, pbv=pbv, vp=vp, i=i, m=m):
                    r = None
                    for hh in range(2):
                        lo = 64 * hh
                        h.matmul(psT[t2][lo:lo + 64, 0:m], lhsT=vp, rhs=pbv[:, hh, 0, 0:m], start=True, stop=False)
                        h.matmul(psT[t2][lo:lo + 64, 0:m], lhsT=vs_tm[0:m, i, 64 * kv:64 * kv + 64], rhs=pbv[0:m, hh, 1, 0:m], start=False, stop=True)
                        h.matmul(psT[t2][lo:lo + 64, 128:128 + m], lhsT=onesb[:, 0:64], rhs=pbv[:, hh, 0, 0:m], start=True, stop=False)
                        r = h.matmul(psT[t2][lo:lo + 64, 128:128 + m], lhsT=onesb[0:m, 0:64], rhs=pbv[0:m, hh, 1, 0:m], start=False, stop=True)
                    return r
                c.op("pe", pvm, reads=[WHB_wh, vpb, VSB[i], CB], writes=[PTB[t2]])
                wf = nxt("tf", 3)
                rr = TF[wf]
                c.op("dve", lambda h, t2=t2, rr=rr, m=m: h.tensor_scalar(out=rr[:, 0:m], in0=psT[t2][:, 128:128 + m], scalar1=esq[:, l, i4:i4 + 1], scalar2=None, op0=ALU.add), reads=[PTB[t2], PB, PB0], writes=[TFB[wf]])
                c.op("dve", lambda h, rr=rr, m=m: h.reciprocal(out=rr[:, 0:m], in_=rr[:, 0:m]), reads=[TFB[wf]], writes=[TFB[wf]])
                c.op("dve", lambda h, t2=t2, rr=rr, m=m, t0=t0: h.tensor_tensor(out=osw[:, t0:t0 + m], in0=psT[t2][:, 0:m], in1=rr[:, 0:m], op=ALU.mult), reads=[PTB[t2], TFB[wf]], writes=[oswb])
            carry = stage1(0)
            for i in range(len(TT)):
                nxt_carry = stage1(i + 1) if i + 1 < len(TT) else None
                stage2(i, carry)
                carry = nxt_carry
            c.op("dve", lambda h, i4=i4: h.tensor_tensor(out=mixT[:, i4, :], in0=osw[:, 0:NT], in1=szt[:, 0:NT], op=ALU.mult), reads=[oswb, szb], writes=[MB[mbuf][i4]])

    def gla_group(l, mbuf):
        for p in range(2):
            wf = nxt("tf", 3)
            c.op("dve", lambda h, p=p, wf=wf: h.scalar_tensor_tensor(out=TF[wf][:, 0:128], in0=S_st[:, p, 0, :], scalar=eG[:, p, 8:9], in1=S_run[:, p, :], op0=ALU.mult, op1=ALU.add), reads=[SSB, EGB, SRB[p]], writes=[TFB[wf]])
            c.dma("sp", lambda h, p=p, wf=wf: h.dma_start(out=o_gla_p[l, 128 * p:128 * p + 128, :], in_=TF[wf][:, 0:128]), reads=[TFB[wf]], writes=[OUTB])
        for p in range(2):
            pm, psm, pb = win_tile(l, [(G_Q + 128 * p, 128)])
            qraw, qrb = WH[0], WHB[0]
            ev2("act", lambda h, d, s, a, b: h.activation(out=d, in_=s, func=AF.Copy, scale=0.125), qraw, pm, psm, pb, writes=[qrb])
            qdec, qdb = WH[1], WHB[1]; kdec, kdb = WH[2], WHB[2]
            for i, (t0, m) in enumerate(TT):
                sp, spb = sp_tile(l, i)
                t = nxt("pt", 3)
                c.op("pe", lambda h, t=t, sp=sp, m=m, p=p: h.matmul(psT[t][:, 0:m], lhsT=sp[0:m, 128 * p:128 * p + 128], rhs=triU[0:m, 0:m], start=True, stop=True), reads=[spb, CB], writes=[PTB[t]])
                wf = nxt("tf", 3)
                e1 = TF[wf]
                c.op("act", lambda h, t=t, e1=e1, m=m: h.activation(out=e1[:, 0:m], in_=psT[t][:, 0:m], func=AF.Exp, scale=-1.0 / 16), reads=[PTB[t]], writes=[TFB[wf]])
                c.op("act", lambda h, t=t, e1=e1, m=m: h.activation(out=e1[:, 128:128 + m], in_=psT[t][:, 0:m], func=AF.Exp, scale=1.0 / 16), reads=[PTB[t]], writes=[TFB[wf]])
                c.op("dve", lambda h, e1=e1, t0=t0, m=m: h.tensor_tensor(out=qdec[:, t0:t0 + m], in0=qraw[:, t0:t0 + m], in1=e1[:, 0:m], op=ALU.mult), reads=[qrb, TFB[wf]], writes=[qdb])
                c.op("dve", lambda h, e1=e1, t0=t0, m=m, p=p: h.tensor_tensor(out=kdec[:, t0:t0 + m], in0=kT_bf[:, p, t0:t0 + m], in1=e1[:, 128:128 + m], op=ALU.mult), reads=[KTB[p], TFB[wf]], writes=[kdb])
            for hh in range(2):
                hd = 2 * p + hh
                lo = 64 * hh
                osb, osbb = WF[2], WFB[2]
                for i, (t0, m) in enumerate(TT):
                    th = nxt("th", 2)
                    sc_, scb = TH[th], THB[th]
                    if i < 8:
                        c.op("dve", lambda h, i=i, sc_=sc_, lo=lo, p=p: h.scalar_tensor_tensor(out=sc_[lo:lo + 64, 0:128], in0=S_st[lo:lo + 64, p, 0, :], scalar=eG[lo:lo + 64, p, i:i + 1], in1=Sloc[lo:lo + 64, p, i, :], op0=ALU.mult, op1=ALU.add), reads=[SSB, EGB, SLB[p][i]], writes=[scb])
                    else:
                        c.op("dve", lambda h, i=i, sc_=sc_, lo=lo, p=p: h.tensor_copy(out=sc_[lo:lo + 64, 0:128], in_=S_st[lo:lo + 64, p, i - 7, :]), reads=[SSB], writes=[scb])
                    c.op("dve", lambda h, sc_=sc_, lo=lo: h.memset(sc_[64 - lo:128 - lo, 0:128], 0.0), writes=[scb])
                    t = nxt("pt", 3)
                    c.op("pe", lambda h, t=t, t0=t0, m=m, lo=lo: h.matmul(psT[t][0:m, 0:m], lhsT=kdec[lo:lo + 64, t0:t0 + m], rhs=qdec[lo:lo + 64, t0:t0 + m], start=True, stop=True), reads=[kdb, qdb], writes=[PTB[t]])
                    c.op("dve", lambda h, t=t, m=m, sc_=sc_: h.tensor_tensor(out=sc_[0:m, 128:128 + m], in0=psT[t][0:m, 0:m], in1=triU[0:m, 0:m], op=ALU.mult), reads=[PTB[t], CB], writes=[scb])
                    t2 = nxt("pt", 3)

                    def om(h, t2=t2, i=i, t0=t0, m=m, sc_=sc_, lo=lo, hd=hd):
                        h.matmul(psT[t2][:, 0:m], lhsT=v_tm[0:m, i, 128 * hd:128 * hd + 128], rhs=sc_[0:m, 128:128 + m], start=True, stop=False)
                        return h.matmul(psT[t2][:, 0:m], lhsT=sc_[:, 0:128], rhs=qdec[:, t0:t0 + m], start=False, stop=True)
                    c.op("pe", om, reads=[VB[i], scb, qdb], writes=[PTB[t2]])
                    c.op("act", lambda h, t2=t2, t0=t0, m=m: h.activation(out=osb[:, t0:t0 + m], in_=psT[t2][:, 0:m], func=AF.Copy), reads=[PTB[t2]], writes=[osbb])
                sq, sqb = WH[3], WHB[3]; rs, rsb = WF[0], WFB[0]
                headnorm(osb, osbb, sq, sqb, rs, rsb, onesb, 128.0)
                c.op("dve", lambda h: h.scalar_tensor_tensor(out=osb[:, 0:NT], in0=osb[:, 0:NT], scalar=gov[:, l:l + 1], in1=rs[:, 0:NT], op0=ALU.mult, op1=ALU.mult), reads=[osbb, rsb, PB, PB0], writes=[osbb])
                pm, psm, pb = win_tile(l, [(G_Z + 128 * hd, 128)])
                szt, szb = WF[1], WFB[1]
                ev2("act", lambda h, d, s, a, b: h.activation(out=d, in_=s, func=AF.Silu), szt, pm, psm, pb, writes=[szb])
                c.op("dve", lambda h, hd=hd: h.tensor_tensor(out=mixT[:, hd, :], in0=osb[:, 0:NT], in1=szt[:, 0:NT], op=ALU.mult), reads=[osbb, szb], writes=[MB[mbuf][hd]])

    try:
        stage(1)
        for l in range(2):
            tiles = [(x_res[:, i, :], 128, XB[i], 128 * i, [HB]) for i in range(2, 8)]
            tiles.append((x_s[:], NST, XSB, NP, [HB]))
            tiles += [(x_res[:, i, :], 128, XB[i], 128 * i, [HB, HBm]) for i in range(2)]

            def mk_a(n):
                sa, m, rb, col0, hb = tiles[n]
                return lambda: norm_a(sa, m, rb, n % 2)

            def mk_b(n):
                sa, m, rb, col0, hb = tiles[n]
                return lambda: norm_b(m, gT, l, col0, n % 2, hb)
            xt = [mk_a(0)]
            for n in range(7):
                xt.append((lambda n=n: (mk_a(n + 1)(), mk_b(n)())))
            mem_kv_prompt(l, xt)
            stage(2 + 20 * l)
            mk_a(8)(); mk_b(7)(); mk_b(8)()
            stage(3 + 20 * l)
            sample_states(l)
            stage(4 + 20 * l)
            tm_pass(l)
            stage(5 + 20 * l)
            gla_chain(l)
            stage(6 + 20 * l)
            conv_group(l, 0)
            stage(7 + 20 * l)
            exchange(l)
            stage(8 + 20 * l)
            wout_group(l, 0, 0)
            stage(9 + 20 * l)
            mem_group(l, 0)
            wout_group(l, 3, 0)
            stage(10 + 20 * l)
            combine(l)
            conv_fix(l, 0)
            stage(11 + 20 * l)
            swa_group(l, 0)
            wout_group(l, 2, 0)
            stage(12 + 20 * l)
            gla_group(l, 0)
            wout_group(l, 1, 0)
            stage(13 + 20 * l)
    except _Stop:
        pass
    for i in range(8):
        c.dma("sp", lambda h, i=i: h.dma_start(out=y_p[128 * i:128 * i + 128, :], in_=x_res[:, i, :]), reads=[XB[i]], writes=[OUTB])
    c.dma("sp", lambda h: h.dma_start(out=y_s, in_=x_s[:]), reads=[XSB], writes=[OUTB])
    def fin(h):
        for i, v in enumerate(c.dval):
            if v > 0:
                h.wait_ge(c.dsem[i], v)
    c._push("sp", fin)
    c.finish()
    if wseq_in is None:
        return nc, WSEQ
    return nc


_NC = None


def kernel(x_prompt, x_sample, mem_prompt, state_conv, state_gla, cache_swa_k, cache_swa_v,
           cache_mem_k, cache_mem_v, g_norm, w_in, conv_w, w_gla_a_up, b_gla_a, g_gla_o,
           g_swa_q, g_swa_k, swa_sinks, g_mem, w_mem_kv, g_mem_q, g_mem_k, w_out):
    global _NC
    f = lambda a: np.ascontiguousarray(np.asarray(a, dtype=np.float32))
    x_prompt, x_sample, mem_prompt = f(x_prompt), f(x_sample), f(mem_prompt)
    state_conv, state_gla = f(state_conv), f(state_gla)
    cache_swa_k, cache_swa_v, cache_mem_k, cache_mem_v = f(cache_swa_k), f(cache_swa_v), f(cache_mem_k), f(cache_mem_v)
    shared = {"g_norm": f(g_norm), "w_in": f(w_in), "conv_w": f(conv_w), "w_up": f(w_gla_a_up), "b_a": f(b_gla_a),
              "g_o": f(g_gla_o), "g_sq": f(g_swa_q), "g_sk": f(g_swa_k), "sinks": f(swa_sinks), "g_mem": f(g_mem),
              "w_mkv": f(w_mem_kv), "g_mq": f(g_mem_q), "g_mk": f(g_mem_k), "w_out": f(w_out)}
    in_maps = []
    for cid in range(8):
        b, j = cid // 4, cid % 4
        sq = slice(4 * cid, 4 * cid + 4)
        cmk = np.zeros((1, 16), np.float32)
        cmk[0, 0:4] = [1.0 if r < j else 0.0 for r in range(4)]
        cmk[0, 4:8] = [1.0 if r == j - 1 else 0.0 for r in range(4)]
        cmk[0, 8] = 1.0 if j > 0 else 0.0
        m = dict(shared)
        m.update({
            "xp": f(x_prompt[b, 1024 * j:1024 * j + 1024]), "xs": f(x_sample[sq].reshape(16, D)), "memp": f(mem_prompt[b]),
            "st_conv": f(state_conv[:, sq]), "st_gla": f(state_gla[:, sq]),
            "c_swk": f(cache_swa_k[:, sq].reshape(2, 4, 128, 128)), "c_swv": f(cache_swa_v[:, sq].reshape(2, 4, 128, 128)),
            "c_mk": f(cache_mem_k[:, sq].reshape(2, 4, 256, 512)), "c_mv": f(cache_mem_v[:, sq].reshape(2, 4, 256, 512)),
            "cmask": cmk})
        in_maps.append(m)
    if _NC is None:
        _, _wseq = build()
        _NC = build(_wseq)
    res = run_bass_kernel_spmd(_NC, in_maps, core_ids=list(range(8))).results
    y_prompt = np.zeros((2, 4096, D), np.float32); y_sample = np.zeros((32, 4, D), np.float32)
    conv_p = np.zeros((2, 2, 2, 512), np.float32); gla_p = np.zeros((2, 2, 4, 64, 128), np.float32)
    swk_p = np.zeros((2, 2, 128, 2, 64), np.float32); swv_p = np.zeros((2, 2, 128, 2, 64), np.float32)
    mk_p = np.zeros((2, 2, 256, 4, 128), np.float32); mv_p = np.zeros((2, 2, 256, 4, 128), np.float32)
    conv_s = np.zeros((2, 32, 2, 512), np.float32); gla_s = np.zeros((2, 32, 4, 64, 128), np.float32)
    swk_s = np.zeros((2, 32, 128, 2, 64), np.float32); swv_s = np.zeros((2, 32, 128, 2, 64), np.float32)
    for cid in range(8):
        b, j = cid // 4, cid % 4
        r = res[cid]
        sq = slice(4 * cid, 4 * cid + 4)
        y_prompt[b, 1024 * j:1024 * j + 1024] = r["y_p"].reshape(1024, D)
        y_sample[sq] = r["y_s"].reshape(4, 4, D)
        conv_s[:, sq] = r["conv_s"].reshape(2, 4, 2, 512); gla_s[:, sq] = r["gla_s"].reshape(2, 4, 4, 64, 128)
        swk_s[:, sq] = r["swk_s"].reshape(2, 4, 128, 2, 64); swv_s[:, sq] = r["swv_s"].reshape(2, 4, 128, 2, 64)
        if j == 3:
            conv_p[:, b] = r["conv_p"].reshape(2, 2, 512); gla_p[:, b] = r["gla_p"].reshape(2, 4, 64, 128)
            swk_p[:, b] = r["swk_p"].reshape(2, 128, 2, 64); swv_p[:, b] = r["swv_p"].reshape(2, 128, 2, 64)
        if j == 0:
            mk_p[:, b] = r["mk_p"].reshape(2, 256, 4, 128); mv_p[:, b] = r["mv_p"].reshape(2, 256, 4, 128)
    return (y_prompt, y_sample, conv_p, gla_p, swk_p, swv_p, mk_p, mv_p, conv_s, gla_s, swk_s, swv_s)
```

```python
import os
import numpy as np
import concourse.bass as bass
import concourse.mybir as mybir
from concourse.bass_utils import run_bass_kernel_spmd

F32 = mybir.dt.float32
BF16 = mybir.dt.bfloat16
ALU = mybir.AluOpType
AF = mybir.ActivationFunctionType
AX = mybir.AxisListType


class Buf:
    __slots__ = ("name", "w", "r")

    def __init__(self, name):
        self.name = name
        self.w = None
        self.r = []


class Ctx:
    ENG = ("pe", "act", "dve", "pool", "sp")

    def __init__(self, nc, n_dma_sems=40, immediate=True):
        self.nc = nc
        self.immediate = immediate
        self.hnd = {"pe": nc.tensor, "act": nc.scalar, "dve": nc.vector, "pool": nc.gpsimd, "sp": nc.sync}
        self.prog = {e: [] for e in self.ENG}
        self.sem = {e: nc.alloc_semaphore("s_" + e) for e in self.ENG}
        self.cnt = {e: 0 for e in self.ENG}
        self.known = {e: {} for e in self.ENG}
        self.dsem = [nc.alloc_semaphore("d%d" % i) for i in range(n_dma_sems)]
        self.dval = [0] * n_dma_sems
        n_pool = n_dma_sems // 2
        self.pool_hist = []
        self.pool_outstanding = int(os.environ.get("KTHROTTLE", "16"))
        self.dpool = {"pool": list(range(0, n_pool)), "sp": list(range(n_pool, n_dma_sems)), "act": list(range(n_pool, n_dma_sems))}
        self.drr = {"pool": 0, "sp": 0, "act": 0}
        self.nbuf = 0

    def buf(self, name=None):
        self.nbuf += 1
        return Buf(name or ("b%d" % self.nbuf))

    def _semh(self, key):
        return self.sem[key[1]] if key[0] == "e" else self.dsem[key[1]]

    def _need(self, eng, deps):
        best = {}
        for key, val in deps:
            if val > best.get(key, 0):
                best[key] = val
        out = []
        kn = self.known[eng]
        for key, val in best.items():
            if kn.get(key, 0) >= val:
                continue
            kn[key] = val
            out.append((key, val))
        return out

    def _deps(self, eng, reads, writes):
        deps = []
        me = ("e", eng)
        for b in reads:
            if b.w is not None:
                deps.append(b.w)
        for b in writes:
            if b.w is not None and not (eng == "pe" and b.w[0] == me):
                deps.append(b.w)
            for r in b.r:
                deps.append(r)
        return deps

    @staticmethod
    def _flat(x):
        out = []
        for b in x:
            if isinstance(b, (list, tuple)):
                out.extend(Ctx._flat(b))
            else:
                out.append(b)
        return out

    def op(self, eng, fn, reads=(), writes=()):
        reads = self._flat(reads); writes = self._flat(writes)
        waits = self._need(eng, self._deps(eng, reads, writes))
        self.cnt[eng] += 1
        val = self.cnt[eng]
        sem = self.sem[eng]
        wl = [(self._semh(k), v) for k, v in waits]

        def emit(h, fn=fn, wl=wl, sem=sem):
            for s, v in wl:
                h.wait_ge(s, v)
            fn(h).then_inc(sem, 1)

        self._push(eng, emit)
        key = ("e", eng)
        for b in reads:
            b.r.append((key, val))
        for b in writes:
            b.w = (key, val)
            b.r = []
        return val

    def dma(self, eng, fn, reads=(), writes=()):
        reads = self._flat(reads); writes = self._flat(writes)
        lst = self.dpool[eng]
        k = "sp" if eng == "act" else eng
        i = lst[self.drr[k]]
        self.drr[k] = (self.drr[k] + 1) % len(lst)
        deps = self._deps(eng, reads, writes)
        key = ("d", i)
        if self.dval[i] > 0:
            deps.append((key, self.dval[i]))
        if eng == "pool" and len(self.pool_hist) >= self.pool_outstanding:
            deps.append(self.pool_hist[-self.pool_outstanding])
        waits = self._need(eng, deps)
        self.dval[i] += 16
        val = self.dval[i]
        sem = self.dsem[i]
        wl = [(self._semh(k), v) for k, v in waits]

        def emit(h, fn=fn, wl=wl, sem=sem):
            for s, v in wl:
                h.wait_ge(s, v)
            fn(h).then_inc(sem, 16)

        self._push(eng, emit)
        if eng == "pool":
            self.pool_hist.append((key, val))
        for b in reads:
            b.r.append((key, val))
        for b in writes:
            b.w = (key, val)
            b.r = []
        return key, val

    def wait_all(self, eng, bufs):
        deps = [b.w for b in bufs if b.w is not None]
        waits = self._need(eng, deps)
        wl = [(self._semh(k), v) for k, v in waits]

        def emit(h, wl=wl):
            for s, v in wl:
                h.wait_ge(s, v)

        self._push(eng, emit)

    def _push(self, eng, emit):
        if self.immediate:
            emit(self.hnd[eng])
        else:
            self.prog[eng].append(emit)

    def finish(self):
        nc = self.nc
        prog = self.prog
        if self.immediate:
            return
        with nc.Block() as block:
            @block.tensor
            def _(h):
                for f in prog["pe"]:
                    f(h)

            @block.scalar
            def _(h):
                for f in prog["act"]:
                    f(h)

            @block.vector
            def _(h):
                for f in prog["dve"]:
                    f(h)

            @block.gpsimd
            def _(h):
                for f in prog["pool"]:
                    f(h)

            @block.sync
            def _(h):
                for f in prog["sp"]:
                    f(h)


D = 2048
NP = 1024
NSQ = 4
NST = 16
NT = NP + NST
DIN = 5904
A_B, A_C, A_H, A_Z = 0, 512, 1024, 1536
G_Q, G_K, G_V, G_A, G_Z = 2048, 2304, 2560, 3072, 3088
S_Q, S_K, S_V, S_Z = 3600, 4112, 4240, 4368
M_Q, M_Z = 4880, 5392
EPS = 1e-6
SEGS = [(0, NP)] + [(NP + 4 * q, 4) for q in range(NSQ)]
TT = [(128 * i, 128) for i in range(8)] + [(NP + 4 * q, 4) for q in range(NSQ)]
PAYC = 528
WORKC = 1044
NSLOT = 4
WSPLIT = 4
SWA_MASK_ENG = os.environ.get("KSWAENG", "dve")


class _Stop(Exception):
    pass


def build(wseq_in=None):
    STAGE = float(os.environ.get("KSTAGE", "99"))

    def stage(n):
        if n == STAGE:
            raise _Stop()
    nc = bass.Bass("TRN2", target_bir_lowering=False)
    c = Ctx(nc, n_dma_sems=72)

    def din(name, shape):
        return nc.dram_tensor(name, list(shape), F32, kind="ExternalInput").ap()

    def dout(name, shape):
        return nc.dram_tensor(name, list(shape), F32, kind="ExternalOutput").ap()

    xp = din("xp", [NP, D]); xs_in = din("xs", [NST, D]); memp = din("memp", [256, D])
    st_conv = din("st_conv", [2, NSQ, 2, 512]); st_gla = din("st_gla", [2, NSQ, 4, 64, 128])
    c_swk = din("c_swk", [2, NSQ, 128, 128]); c_swv = din("c_swv", [2, NSQ, 128, 128])
    c_mk = din("c_mk", [2, NSQ, 256, 512]); c_mv = din("c_mv", [2, NSQ, 256, 512])
    g_norm = din("g_norm", [2, D]); w_in = din("w_in", [2, D, DIN]); conv_w = din("conv_w", [2, 3, 512])
    w_up = din("w_up", [2, 16, 256]); b_a = din("b_a", [2, 256]); g_o = din("g_o", [2, 128])
    g_sq = din("g_sq", [2, 64]); g_sk = din("g_sk", [2, 64]); sinks = din("sinks", [2, 8])
    g_mem = din("g_mem", [2, D]); w_mkv = din("w_mkv", [2, D, 1024]); g_mq = din("g_mq", [2, 128])
    g_mk = din("g_mk", [2, 128]); w_out = din("w_out", [2, D, D]); cmask = din("cmask", [1, 16])

    y_p = dout("y_p", [NP, D]); y_s = dout("y_s", [NST, D])
    o_conv_p = dout("conv_p", [2, 2, 512]); o_gla_p = dout("gla_p", [2, 256, 128])
    o_swk_p = dout("swk_p", [2, 128, 128]); o_swv_p = dout("swv_p", [2, 128, 128])
    o_mk_p = dout("mk_p", [2, 256, 512]); o_mv_p = dout("mv_p", [2, 256, 512])
    o_conv_s = dout("conv_s", [2, NSQ, 2, 512]); o_gla_s = dout("gla_s", [2, NSQ, 256, 128])
    o_swk_s = dout("swk_s", [2, NSQ, 128, 128]); o_swv_s = dout("swv_s", [2, NSQ, 128, 128])
    agin = [nc.dram_tensor("agin%d" % l, [128, PAYC], F32) for l in range(2)]
    agout = [nc.dram_tensor("agout%d" % l, [4 * 128, PAYC], F32) for l in range(2)]
    OUTB = c.buf("outputs")
    AGB = [(c.buf(), c.buf()) for _ in range(2)]

    A = nc.alloc_sbuf_tensor
    x_res = A("x_res", [128, 8, D], F32); XB = [c.buf("x%d" % i) for i in range(8)]
    x_s = A("x_s", [NST, D], F32); XSB = c.buf("x_s")
    hnT = A("hnT", [128, 16, NT], BF16); HB = c.buf("hnT"); HBm = c.buf("hnTm")
    mixT = A("mixT", [128, 4, NT], BF16); MBall = [c.buf("mix%d" % j) for j in range(4)]
    MB = [MBall, MBall]
    mixflat = mixT[:].rearrange("p b n -> p (b n)")
    xsb = [mixflat[:, 0:2048], mixflat[:, 2048:4096]]
    XSBB = [c.buf("xsb0"), c.buf("xsb1")]
    xs_first = {0: True, 1: True}
    wring = A("wring", [128, NSLOT, 16, 128], BF16); WB = [[c.buf("w%d_%d" % (i, j)) for j in range(16)] for i in range(NSLOT)]
    v_tm = A("v_tm", [128, 12, 512], BF16); VB = [c.buf() for _ in range(12)]
    vs_tm = A("vs_tm", [128, 12, 128], BF16); VSB = [c.buf() for _ in range(12)]
    kT_bf = A("kT_bf", [128, 2, NT], BF16); KTB = [c.buf(), c.buf()]
    kn = A("kn", [128, 2, NT], BF16); KNB = [c.buf(), c.buf()]
    Sloc = A("Sloc", [128, 2, 8, 128], BF16); SLB = [[c.buf() for _ in range(8)] for _ in range(2)]
    a_ext = A("a_ext", [17, NT], BF16); AEB = c.buf("a_ext")
    WH = [A("wh%d" % i, [128, WORKC], BF16) for i in range(4)]; WHB = [c.buf("wh%d" % i) for i in range(4)]
    WFall = A("wfall", [128, 3, WORKC], F32)
    WF = [WFall[:, i, :] for i in range(3)]; WFB = [c.buf("wf%d" % i) for i in range(3)]
    memstage = WFall[:].rearrange("p a n -> p (a n)")[:, 0:2048]
    memK = WH[3][:, 0:1024].rearrange("p (a b) -> p a b", a=4); MKB = WHB[3]
    memV = WH[2][:, 0:1024].rearrange("p (a b) -> p a b", a=2); MVB = WHB[2]
    TF = [A("tf%d" % i, [128, PAYC], F32) for i in range(3)]; TFB = [c.buf("tf%d" % i) for i in range(3)]
    TH = [A("th%d" % i, [128, 1024], BF16) for i in range(2)]; THB = [c.buf("th%d" % i) for i in range(2)]
    ident = A("ident", [128, 128], BF16)
    triU = A("triU", [128, 128], F32); triUb = A("triUb", [128, 128], BF16)
    swm = A("swm", [128, 2, 2, 128], BF16); swm1 = A("swm1", [128, 2, 2, 128], BF16)
    onesb = A("onesb", [128, 128], BF16); bdiag = A("bdiag", [128, 128], BF16)
    CB = c.buf("consts")
    gT = A("gT", [128, 2, 16], F32); gmT = A("gmT", [128, 2, 16], F32)
    cw = A("cw", [128, 2, 3, 4], F32); wupb = A("wupb", [17, 2, 256], BF16)
    gov = A("gov", [128, 2], F32); gsq = A("gsq", [128, 2], F32); gsk = A("gsk", [128, 2], F32)
    gmq = A("gmq", [128, 2], F32); gmkb = A("gmkb", [128, 2, 128], F32)
    esk = A("esk", [128, 2, 8], F32); esq = A("esq", [128, 2, 4], F32)
    cm = A("cm", [128, 16], F32)
    PB = c.buf("params")
    sm = A("sm", [128, 64], F32)
    SMB = [c.buf() for _ in range(64)]
    fix_t1 = A("fix_t1", [128, 4, 5, 2], F32); fix_bz = A("fix_bz", [128, 4, 5, 2], F32)
    fix_lk = A("fix_lk", [128, 4, 5, 2], F32); ulast = A("ulast", [128, 4, 5, 2], F32)
    cprev = A("cprev", [128, 4, 5, 2], F32); FXB = c.buf("fix"); CPB = c.buf("cprev")
    S_run = A("S_run", [128, 2, 128], F32); SRB = [c.buf(), c.buf()]
    S_st = A("S_st", [128, 2, 5, 128], F32); SSB = c.buf("S_st")
    eG = A("eG", [128, 2, 9], F32); EGB = c.buf("eG")
    kprev = A("kprev", [128, 2, 5, 128], BF16); KPB = [c.buf() for _ in range(5)]
    vprev = A("vprev", [128, 5, 128], BF16); VPB = [c.buf() for _ in range(5)]
    pay = A("pay", [128, PAYC], F32); PYB = c.buf("pay")
    qn_s = A("qn_s", [128, 4, NST], BF16); sz_s = A("sz_s", [128, 4, NST], BF16); QSB = c.buf("qn_s")
    psA = nc.alloc_psum_tensor("psA", [128, 1024], F32); psB = nc.alloc_psum_tensor("psB", [128, 1024], F32)
    psS = nc.alloc_psum_tensor("psS", [128, 512], F32)
    psT = [nc.alloc_psum_tensor("psT%d" % i, [128, 512], F32) for i in range(3)]
    PAB = [c.buf("psA"), c.buf("psB")]; PSB = [c.buf() for _ in range(32)]; PTB = [c.buf() for _ in range(3)]
    PA = [psA, psB]
    st = {"ps": 0, "pss": 0, "pt": 0, "w": 0, "tf": 0, "th": 0, "sm": 0, "wk": 0, "wi": 0}
    WSEQ = list(wseq_in) if wseq_in is not None else []

    def nxt(k, n):
        v = st[k]; st[k] = (v + 1) % n; return v

    def smcol():
        i = nxt("sm", 64); return sm[:, i:i + 1], SMB[i]

    def materialize(spec):
        kind = spec[0]
        if kind == "win":
            _, l, cols = spec
            wv = w_in[l].rearrange("(kt p) c -> p kt c", p=128)
            dm = []; off = 0
            for c0, n in cols:
                dm.append((lambda sl, off=off, n=n: sl[:, :, off:off + n], wv[:, :, c0:c0 + n])); off += n
            return dm
        if kind == "mkv":
            _, l, j = spec
            wv = w_mkv[l].rearrange("(kt p) c -> p kt c", p=128)
            return [(lambda sl: sl, wv[:, :, 128 * j:128 * j + 128])]
        _, l, g, half = spec
        src = w_out[l][g * 512:(g + 1) * 512, half * 512:(half + 1) * 512].rearrange("(kt p) c -> p kt c", p=128)
        return [(lambda sl: sl.rearrange("p a b -> p (a b)").rearrange("p (kt c) -> p kt c", kt=4), src)]

    wchunk = [0]

    def wissue(j):
        s = j % NSLOT
        wchunk[0] = 0
        for dv, src in materialize(WSEQ[j]):
            dst = dv(wring[:, s])
            nk = dst.shape[1]
            step = max(1, nk // WSPLIT)
            for k0 in range(0, nk, step):
                c.dma("pool", lambda h, dst=dst, src=src, k0=k0, step=step: h.dma_start(out=dst[:, k0:k0 + step, :], in_=src[:, k0:k0 + step, :]), writes=[WB[s][wchunk[0] % 16]])
                wchunk[0] += 1

    def wslot_load(spec):
        kk = st["wk"]; st["wk"] += 1
        if wseq_in is None:
            WSEQ.append(spec)
        else:
            assert WSEQ[kk] == spec, (kk, spec, WSEQ[kk])
        while st["wi"] <= min(kk + (NSLOT - 1 if wseq_in is not None else 0), len(WSEQ) - 1):
            wissue(st["wi"]); st["wi"] += 1
        s = kk % NSLOT
        return wring[:, s], WB[s]

    def win_cols(l, cols):
        wsl, wb = wslot_load(("win", l, tuple(cols)))
        return wsl, wb, sum(n for _, n in cols)

    def win_tile(l, cols):
        wsl, wb, M = win_cols(l, cols)
        a = nxt("ps", 2); ss = nxt("pss", 32)
        P = PA[a]

        def mm(h):
            r = None
            for kt in range(16):
                h.matmul(P[0:M, 0:512], lhsT=wsl[:, kt, 0:M], rhs=hnT[:, kt, 0:512], start=(kt == 0), stop=(kt == 15))
                r = h.matmul(P[0:M, 512:1024], lhsT=wsl[:, kt, 0:M], rhs=hnT[:, kt, 512:1024], start=(kt == 0), stop=(kt == 15))
            return r
        c.op("pe", mm, reads=[wb, HB], writes=[PAB[a]])

        def mms(h):
            r = None
            for kt in range(16):
                r = h.matmul(psS[0:M, ss * 16:(ss + 1) * 16], lhsT=wsl[:, kt, 0:M], rhs=hnT[:, kt, NP:NT], start=(kt == 0), stop=(kt == 15))
            return r
        c.op("pe", mms, reads=[wb, HB], writes=[PSB[0]])
        return P[0:M, :], psS[0:M, ss * 16:(ss + 1) * 16], [PAB[a], PSB[0]]

    def ev2(eng, fn, dst, pm, psm, pb, reads=(), writes=()):
        c.op(eng, lambda h: fn(h, dst[:, 0:NP], pm, 0, NP), reads=[pb[0]] + list(reads), writes=list(writes))
        c.op(eng, lambda h: fn(h, dst[:, NP:NT], psm, NP, NT), reads=[pb[1]] + list(reads), writes=list(writes))

    def rstd_from(eng_ss_ap, n_mean, out_ap, reads, writes):
        c.op("act", lambda h: h.activation(out=out_ap, in_=eng_ss_ap, func=AF.Ln, bias=EPS, scale=1.0 / n_mean), reads=reads, writes=writes)
        c.op("act", lambda h: h.activation(out=out_ap, in_=out_ap, func=AF.Exp, scale=-0.5), reads=writes, writes=writes)

    identf = TF[0][:, 0:128]
    c.op("pool", lambda h: h.memset(identf, 0.0), writes=[CB, TFB[0]])
    c.op("pool", lambda h: h.affine_select(out=identf, in_=identf, pattern=[[-1, 128]], compare_op=ALU.not_equal, fill=1.0, base=0, channel_multiplier=1), reads=[CB], writes=[CB])
    c.op("pool", lambda h: h.memset(triU[:], 1.0), writes=[CB])
    c.op("pool", lambda h: h.affine_select(out=triU[:], in_=triU[:], pattern=[[1, 128]], compare_op=ALU.is_ge, fill=0.0, base=0, channel_multiplier=-1), reads=[CB], writes=[CB])
    c.op("dve", lambda h: h.tensor_copy(out=ident[:], in_=identf), reads=[CB, TFB[0]], writes=[CB])
    c.op("dve", lambda h: h.tensor_copy(out=triUb[:], in_=triU[:]), reads=[CB], writes=[CB])
    c.op("dve", lambda h: h.memset(onesb[:], 1.0), writes=[CB])
    c.op("dve", lambda h: h.memset(bdiag[:], 0.0), writes=[CB])
    c.op("dve", lambda h: h.memset(bdiag[0:64, 0:64], 1.0), writes=[CB])
    c.op("dve", lambda h: h.memset(bdiag[64:128, 64:128], 1.0), writes=[CB])
    for hh in range(2):
        c.op("dve", lambda h, hh=hh: h.tensor_scalar(out=swm[:, hh, 0, :], in0=triU[:], scalar1=-1.0, scalar2=1.0, op0=ALU.mult, op1=ALU.add), reads=[CB], writes=[CB])
        c.op("dve", lambda h, hh=hh: h.tensor_copy(out=swm[:, hh, 1, :], in_=triU[:]), reads=[CB], writes=[CB])
    PB0 = c.buf("params0")
    NCg = dict(allow_slow_non_contiguous=True)

    def pdma(out_ap, in_ap, buf=None, **kw):
        c.dma("sp", lambda h: h.dma_start(out=out_ap, in_=in_ap, **kw), writes=[buf or PB])

    def params_first():
        stg = TF[1]; stb = TFB[1]
        rows = [(g_norm.rearrange("l (kt p) -> (l kt) p", p=128), 0, 32, 0, 128),
                (g_mem.rearrange("l (kt p) -> (l kt) p", p=128), 32, 32, 0, 128),
                (conv_w.rearrange("l j (ct p) -> (l j ct) p", p=128), 64, 24, 0, 128),
                (g_o, 88, 2, 0, 128), (g_mq, 90, 2, 0, 128),
                (g_sq, 92, 2, 0, 64), (g_sq, 92, 2, 64, 64), (g_sk, 94, 2, 0, 64), (g_sk, 94, 2, 64, 64)]
        for src_ap, r0, nr, c0, nc_ in rows:
            c.dma("sp", lambda h, src_ap=src_ap, r0=r0, nr=nr, c0=c0, nc_=nc_: h.dma_start(out=stg[r0:r0 + nr, c0:c0 + nc_], in_=src_ap), writes=[stb])
        pdma(gmkb[:, 0, :], g_mk[0:1, :].partition_broadcast(128), PB0)
        c.op("pe", lambda h: h.matmul(psT[0][:, 0:96], lhsT=stg[0:96, 0:128], rhs=identf[0:96, 0:96], start=True, stop=True), reads=[stb, TFB[0], CB], writes=[PTB[0]])
        pv = psT[0]
        c.op("dve", lambda h: h.tensor_copy(out=gT[:].rearrange("p l k -> p (l k)"), in_=pv[:, 0:32]), reads=[PTB[0]], writes=[PB0])
        c.op("dve", lambda h: h.tensor_copy(out=gmT[:].rearrange("p l k -> p (l k)"), in_=pv[:, 32:64]), reads=[PTB[0]], writes=[PB0])
        c.op("dve", lambda h: h.tensor_copy(out=cw[:].rearrange("p l j c -> p (l j c)"), in_=pv[:, 64:88]), reads=[PTB[0]], writes=[PB0])
        c.op("dve", lambda h: h.tensor_copy(out=gov[:], in_=pv[:, 88:90]), reads=[PTB[0]], writes=[PB0])
        c.op("dve", lambda h: h.tensor_copy(out=gmq[:], in_=pv[:, 90:92]), reads=[PTB[0]], writes=[PB0])
        c.op("dve", lambda h: h.tensor_copy(out=gsq[:], in_=pv[:, 92:94]), reads=[PTB[0]], writes=[PB0])
        c.op("dve", lambda h: h.tensor_copy(out=gsk[:], in_=pv[:, 94:96]), reads=[PTB[0]], writes=[PB0])

    def load_x():
        for i in range(8):
            c.dma("sp", lambda h, i=i: h.dma_start(out=x_res[:, i, :], in_=xp[128 * i:128 * i + 128, :]), writes=[XB[i]])
        c.dma("sp", lambda h: h.dma_start(out=x_s[:], in_=xs_in), writes=[XSB])

    def params_rest():
        pdma(cm[:], cmask.partition_broadcast(128))
        for l in range(2):
            if l > 0:
                pdma(gmkb[:, l, :], g_mk[l:l + 1, :].partition_broadcast(128))
            c.dma("pool", lambda h, l=l: h.dma_start(out=wupb[0:16, l, :], in_=w_up[l]), writes=[PB])
            c.dma("pool", lambda h, l=l: h.dma_start(out=wupb[16:17, l, :], in_=b_a[l:l + 1, :]), writes=[PB])
            pdma(esk[:, l, :], sinks[l:l + 1, :].partition_broadcast(128))
        c.op("act", lambda h: h.activation(out=esk[:], in_=esk[:], func=AF.Exp), reads=[PB, PB0], writes=[PB])
        for l in range(2):
            for i4 in range(4):
                c.op("dve", lambda h, l=l, i4=i4: h.tensor_copy(out=esq[0:64, l, i4:i4 + 1], in_=esk[0:64, l, 2 * i4:2 * i4 + 1]), reads=[PB, PB0], writes=[PB])
                c.op("dve", lambda h, l=l, i4=i4: h.tensor_copy(out=esq[64:128, l, i4:i4 + 1], in_=esk[64:128, l, 2 * i4 + 1:2 * i4 + 2]), reads=[PB, PB0], writes=[PB])
        c.op("dve", lambda h: h.tensor_copy(out=swm1[:], in_=swm[:]), reads=[CB], writes=[CB])
        for hh in range(2):
            c.op("dve", lambda h, hh=hh: h.tensor_scalar(out=swm1[:, hh, 0, :], in0=swm[:, hh, 0, :], scalar1=cm[:, 8:9], scalar2=None, op0=ALU.mult), reads=[CB, PB, PB0], writes=[CB])

    c.op("dve", lambda h: h.memset(a_ext[:], 1.0), writes=[AEB])
    c.op("dve", lambda h: h.memset(pay[:], 0.0), writes=[PYB])
    params_first()
    start_hooks = {0: load_x, 1: params_rest}

    def norm_a(src_ap, m, rb, par):
        xb = xsb[par]
        ssc, ssb = smcol()
        wl = [XSBB[par]] + (MBall if xs_first[par] else [])
        xs_first[par] = False
        c.op("act", lambda h: h.activation(out=xb[0:m, :], in_=src_ap, func=AF.Square, accum_out=ssc[0:m, :]), reads=[rb], writes=wl + [ssb])
        rstd_from(ssc[0:m, :], float(D), ssc[0:m, :], [ssb], [ssb])
        c.op("dve", lambda h: h.tensor_scalar(out=xb[0:m, :], in0=src_ap, scalar1=ssc[0:m, :], scalar2=None, op0=ALU.mult), reads=[rb, ssb], writes=[XSBB[par]])

    def norm_b(m, gsrc, l, col0, par, hbufs=None):
        hbufs = hbufs or [HB]
        xb = xsb[par]
        for g4 in range(4):
            t = nxt("pt", 3)
            pv = psT[t][:].bitcast(BF16)

            def tr(h, g4=g4, pv=pv):
                r = None
                for j in range(4):
                    kt = g4 * 4 + j
                    r = h.transpose(out=pv[:, j * 128:j * 128 + m], in_=xb[0:m, kt * 128:(kt + 1) * 128], identity=ident[0:m, 0:m])
                return r
            c.op("pe", tr, reads=[XSBB[par]] + MBall + [CB], writes=[PTB[t]])
            src = pv[:, 0:512].rearrange("p (j n) -> p j n", j=4)[:, :, 0:m]
            c.op("dve", lambda h, g4=g4, src=src: h.tensor_tensor(out=hnT[:, g4 * 4:g4 * 4 + 4, col0:col0 + m], in0=src,
                 in1=gsrc[:, l, g4 * 4:g4 * 4 + 4].unsqueeze(2).to_broadcast([128, 4, m]), op=ALU.mult), reads=[PTB[t], PB, PB0], writes=hbufs)

    def norm_transpose(src_ap, m, rb, gsrc, l, col0, par, hbufs=None):
        norm_a(src_ap, m, rb, par)
        norm_b(m, gsrc, l, col0, par, hbufs)

    def mem_kv_prompt(l, inter=()):
        inter = list(inter)
        xs_first[0] = True; xs_first[1] = True
        for mt in range(2):
            c.dma("sp", lambda h, mt=mt: h.dma_start(out=memstage, in_=memp[128 * mt:128 * mt + 128, :]), writes=[WFB[0], WFB[1]])
            if l == 0:
                start_hooks[mt]()
            norm_transpose(memstage, 128, WFB[0], gmT, l, 128 * mt, mt, [HBm, HB])
            stage(1.5)
        wv = w_mkv[l].rearrange("(kt p) c -> p kt c", p=128)
        kst, kstb = WF[0], WFB[0]
        vst, vstb = WF[1], WFB[1]
        for j in range(8):
            wsl, wb = wslot_load(("mkv", l, j))
            if j == 1:
                stage(1.7)
            if j == 2:
                stage(1.75)
            if j == 4:
                stage(1.8)
            if j == 5:
                stage(1.85)
            for mt in range(2):
                t = nxt("pt", 3)

                def mm(h, mt=mt, t=t, wsl=wsl):
                    r = None
                    for kt in range(16):
                        r = h.matmul(psT[t][:, 0:128], lhsT=hnT[:, kt, 128 * mt:128 * mt + 128], rhs=wsl[:, kt, :], start=(kt == 0), stop=(kt == 15))
                    return r
                c.op("pe", mm, reads=[wb, HBm], writes=[PTB[t]])
                if j < 4:
                    hd = j
                    dst = kst[:, mt * 512 + hd * 128: mt * 512 + hd * 128 + 128]
                    ssc, ssb = smcol()
                    c.op("act", lambda h, dst=dst, t=t, ssc=ssc: h.activation(out=dst, in_=psT[t][:, 0:128], func=AF.Square, accum_out=ssc), reads=[PTB[t]], writes=[kstb, ssb])
                    rstd_from(ssc, 128.0, ssc, [ssb], [ssb])
                    c.op("dve", lambda h, dst=dst, t=t, ssc=ssc: h.scalar_tensor_tensor(out=dst, in0=psT[t][:, 0:128], scalar=ssc, in1=gmkb[:, l, :], op0=ALU.mult, op1=ALU.mult), reads=[PTB[t], ssb, PB, PB0], writes=[kstb])
                    wh = nxt("th", 2)
                    c.op("dve", lambda h, dst=dst, wh=wh: h.tensor_copy(out=TH[wh][:, 0:128], in_=dst), reads=[kstb], writes=[THB[wh]])
                    t2 = nxt("pt", 3)
                    pv = psT[t2][:].bitcast(BF16)
                    c.op("pe", lambda h, wh=wh, pv=pv: h.transpose(out=pv[:, 0:128], in_=TH[wh][:, 0:128], identity=ident[:]), reads=[THB[wh], CB], writes=[PTB[t2]])
                    c.op("act", lambda h, pv=pv, hd=hd, mt=mt: h.activation(out=memK[:, hd, 128 * mt:128 * mt + 128], in_=pv[:, 0:128], func=AF.Copy), reads=[PTB[t2]], writes=[MKB])
                else:
                    hd = j - 4
                    dst = vst[:, mt * 512 + hd * 128: mt * 512 + hd * 128 + 128]
                    c.op("act", lambda h, dst=dst, t=t: h.activation(out=dst, in_=psT[t][:, 0:128], func=AF.Copy), reads=[PTB[t]], writes=[vstb])
                    c.op("dve", lambda h, dst=dst, hd=hd, mt=mt: h.tensor_copy(out=memV[:, mt, hd * 128:hd * 128 + 128], in_=dst), reads=[vstb], writes=[MVB])
            if inter:
                inter.pop(0)()
        while inter:
            inter.pop(0)()
        stage(1.9)
        c.dma("sp", lambda h: h.dma_start(out=o_mk_p[l].rearrange("(mt p) f -> p mt f", p=128), in_=kst[:, 0:1024].rearrange("p (mt f) -> p mt f", mt=2)), reads=[kstb], writes=[OUTB])
        c.dma("sp", lambda h: h.dma_start(out=o_mv_p[l].rearrange("(mt p) f -> p mt f", p=128), in_=vst[:, 0:1024].rearrange("p (mt f) -> p mt f", mt=2)), reads=[vstb], writes=[OUTB])

    def wout_group(l, g, mbuf):
        for half in range(4):
            wsl, wb = wslot_load(("wout", l, g, half))
            wv = wsl.rearrange("p a b -> p (a b)").rearrange("p (kt c) -> p kt c", kt=4)
            for i, (c0, m) in enumerate(TT[:8] + [(NP, NST)]):
                t = nxt("pt", 3)

                def mm(h, c0=c0, m=m, t=t, wv=wv):
                    r = None
                    for kt in range(4):
                        r = h.matmul(psT[t][0:m, :], lhsT=mixT[:, kt, c0:c0 + m], rhs=wv[:, kt, :], start=(kt == 0), stop=(kt == 3))
                    return r
                c.op("pe", mm, reads=[wb] + MB[mbuf], writes=[PTB[t]])
                if i < 8:
                    dst = x_res[:, i, half * 512:(half + 1) * 512]; db = XB[i]
                else:
                    dst = x_s[:, half * 512:(half + 1) * 512]; db = XSB
                c.op("dve", lambda h, dst=dst, t=t, m=m: h.tensor_tensor(out=dst, in0=psT[t][0:m, :], in1=dst, op=ALU.add), reads=[PTB[t], db], writes=[db])

    def conv_group(l, mbuf):
        c.op("dve", lambda h: h.memset(WF[1][:, 0:2], 0.0), writes=[WFB[1]])
        for ct in range(4):
            cE, cEb = WF[0], WFB[0]; ext, extb = WF[1], WFB[1]; t1, t1b = WF[2], WFB[2]; bz, bzb = WH[1], WHB[1]
            szt, szb = WH[0], WHB[0]
            pm, psm, pb = win_tile(l, [(A_C + 128 * ct, 128)])
            ev2("act", lambda h, d, s, a, b: h.activation(out=d, in_=s, func=AF.Copy), cE, pm, psm, pb, writes=[cEb])
            pm, psm, pb = win_tile(l, [(A_H + 128 * ct, 128)])
            ev2("dve", lambda h, d, s, a, b: h.tensor_tensor(out=d, in0=s, in1=cE[:, a:b], op=ALU.mult), ext[:, 2:2 + NT], pm, psm, pb, reads=[cEb], writes=[extb])
            c.op("dve", lambda h, ct=ct: h.tensor_scalar(out=t1[:, 0:NT], in0=ext[:, 0:NT], scalar1=cw[:, l, 0, ct:ct + 1], scalar2=None, op0=ALU.mult), reads=[extb, PB, PB0], writes=[t1b])
            c.op("dve", lambda h, ct=ct: h.scalar_tensor_tensor(out=t1[:, 0:NT], in0=ext[:, 1:1 + NT], scalar=cw[:, l, 1, ct:ct + 1], in1=t1[:, 0:NT], op0=ALU.mult, op1=ALU.add), reads=[extb, PB, t1b], writes=[t1b])
            c.op("dve", lambda h, ct=ct: h.scalar_tensor_tensor(out=t1[:, 0:NT], in0=ext[:, 2:2 + NT], scalar=cw[:, l, 2, ct:ct + 1], in1=t1[:, 0:NT], op0=ALU.mult, op1=ALU.add), reads=[extb, PB, t1b], writes=[t1b])
            pm, psm, pb = win_tile(l, [(A_Z + 128 * ct, 128)])
            ev2("act", lambda h, d, s, a, b: h.activation(out=d, in_=s, func=AF.Silu), szt, pm, psm, pb, writes=[szb])
            pm, psm, pb = win_tile(l, [(A_B + 128 * ct, 128)])
            ev2("dve", lambda h, d, s, a, b: h.tensor_tensor(out=d, in0=s, in1=szt[:, a:b], op=ALU.mult), bz, pm, psm, pb, reads=[szb], writes=[bzb])
            c.op("dve", lambda h, ct=ct: h.tensor_tensor(out=mixT[:, ct, :], in0=bz[:, 0:NT], in1=t1[:, 0:NT], op=ALU.mult), reads=[bzb, t1b], writes=[MB[mbuf][ct]])
            for si, (s0, n) in enumerate(SEGS):
                c.op("act", lambda h, ct=ct, si=si, s0=s0: h.copy(out=fix_t1[:, ct, si, :], in_=t1[:, s0:s0 + 2]), reads=[t1b], writes=[FXB])
                c.op("act", lambda h, ct=ct, si=si, s0=s0: h.copy(out=fix_bz[:, ct, si, :], in_=bz[:, s0:s0 + 2]), reads=[bzb], writes=[FXB])
                c.op("act", lambda h, ct=ct, si=si, s0=s0: h.copy(out=fix_lk[:, ct, si, :], in_=ext[:, s0:s0 + 2]), reads=[extb], writes=[FXB])
                c.op("act", lambda h, ct=ct, si=si, s0=s0, n=n: h.copy(out=ulast[:, ct, si, :], in_=ext[:, s0 + n:s0 + n + 2]), reads=[extb], writes=[FXB])

    def conv_fix(l, mbuf):
        w0 = cw[:, l, 0, :].unsqueeze(2).to_broadcast([128, 4, 5])
        w1 = cw[:, l, 1, :].unsqueeze(2).to_broadcast([128, 4, 5])
        c.op("dve", lambda h: h.tensor_tensor(out=cprev[:], in0=cprev[:], in1=fix_lk[:], op=ALU.subtract), reads=[CPB, FXB], writes=[CPB, PB0])
        c.op("dve", lambda h: h.tensor_tensor(out=fix_lk[:, :, :, 0], in0=cprev[:, :, :, 0], in1=w0, op=ALU.mult), reads=[CPB, PB, PB0], writes=[FXB])
        c.op("dve", lambda h: h.tensor_tensor(out=fix_lk[:, :, :, 1], in0=cprev[:, :, :, 1], in1=w1, op=ALU.mult), reads=[CPB, PB, FXB], writes=[FXB])
        c.op("dve", lambda h: h.tensor_tensor(out=fix_t1[:, :, :, 0], in0=fix_lk[:, :, :, 0], in1=fix_lk[:, :, :, 1], op=ALU.add), reads=[FXB], writes=[FXB])
        c.op("dve", lambda h: h.tensor_tensor(out=fix_t1[:, :, :, 1], in0=cprev[:, :, :, 1], in1=w0, op=ALU.mult), reads=[CPB, PB, FXB], writes=[FXB])
        c.op("dve", lambda h: h.tensor_tensor(out=fix_t1[:], in0=fix_t1[:], in1=fix_bz[:], op=ALU.mult), reads=[FXB], writes=[FXB])
        dS = qn_s; dP = sz_s
        c.op("dve", lambda h: h.memset(dS[:], 0.0), writes=[QSB])
        c.op("dve", lambda h: h.tensor_copy(out=dP[:, :, 0:2], in_=fix_t1[:, :, 0, :]), reads=[FXB], writes=[QSB])
        for q in range(NSQ):
            c.op("dve", lambda h, q=q: h.tensor_copy(out=dS[:, :, 4 * q:4 * q + 2], in_=fix_t1[:, :, 1 + q, :]), reads=[FXB], writes=[QSB])
        for half in range(4):
            wsl, wb = wslot_load(("wout", l, 0, half))
            wv = wsl.rearrange("p a b -> p (a b)").rearrange("p (kt c) -> p kt c", kt=4)
            t = nxt("pt", 3)

            def mmp(h, t=t, wv=wv):
                r = None
                for kt in range(4):
                    r = h.matmul(psT[t][0:2, :], lhsT=dP[:, kt, 0:2], rhs=wv[:, kt, :], start=(kt == 0), stop=(kt == 3))
                return r
            c.op("pe", mmp, reads=[wb, QSB], writes=[PTB[t]])
            dst = x_res[0:2, 0, half * 512:(half + 1) * 512]
            c.op("dve", lambda h, dst=dst, t=t: h.tensor_tensor(out=dst, in0=psT[t][0:2, :], in1=dst, op=ALU.add), reads=[PTB[t], XB[0]], writes=[XB[0]])
            t = nxt("pt", 3)

            def mms_(h, t=t, wv=wv):
                r = None
                for kt in range(4):
                    r = h.matmul(psT[t][0:NST, :], lhsT=dS[:, kt, :], rhs=wv[:, kt, :], start=(kt == 0), stop=(kt == 3))
                return r
            c.op("pe", mms_, reads=[wb, QSB], writes=[PTB[t]])
            dst2 = x_s[:, half * 512:(half + 1) * 512]
            c.op("dve", lambda h, dst2=dst2, t=t: h.tensor_tensor(out=dst2, in0=psT[t][0:NST, :], in1=dst2, op=ALU.add), reads=[PTB[t], XSB], writes=[XSB])
        for j in range(2):
            c.dma("sp", lambda h, j=j: h.dma_start(out=o_conv_p[l, j:j + 1, :].rearrange("o (ct p) -> p (o ct)", p=128), in_=ulast[:, :, 0, j], **NCg), reads=[FXB], writes=[OUTB])
            for q in range(NSQ):
                c.dma("sp", lambda h, j=j, q=q: h.dma_start(out=o_conv_s[l, q, j:j + 1, :].rearrange("o (ct p) -> p (o ct)", p=128), in_=ulast[:, :, 1 + q, j], **NCg), reads=[FXB], writes=[OUTB])

    def tm_pass(l):
        wv = w_in[l].rearrange("(kt p) c -> p kt c", p=128)
        for j in range(5):
            c0 = G_V + 128 * j if j < 4 else S_V
            wsl, wb = wslot_load(("win", l, ((c0, 128),)))
            for i, (t0, m) in enumerate(TT):
                t = nxt("pt", 3)

                def mm(h, t0=t0, m=m, t=t, wsl=wsl):
                    r = None
                    for kt in range(16):
                        r = h.matmul(psT[t][0:m, 0:128], lhsT=hnT[:, kt, t0:t0 + m], rhs=wsl[:, kt, :], start=(kt == 0), stop=(kt == 15))
                    return r
                c.op("pe", mm, reads=[wb, HB], writes=[PTB[t]])
                if j < 4:
                    c.op("act", lambda h, i=i, j=j, m=m, t=t: h.activation(out=v_tm[0:m, i, 128 * j:128 * j + 128], in_=psT[t][0:m, 0:128], func=AF.Copy), reads=[PTB[t]], writes=[VB[i]])
                else:
                    c.op("act", lambda h, i=i, m=m, t=t: h.activation(out=vs_tm[0:m, i, :], in_=psT[t][0:m, 0:128], func=AF.Copy), reads=[PTB[t]], writes=[VSB[i]])
                    if i == 7:
                        c.op("act", lambda h, t=t: h.activation(out=pay[:, 386:514], in_=psT[t][:, 0:128], func=AF.Copy), reads=[PTB[t]], writes=[PYB])
                    if i >= 8:
                        q = i - 8
                        wf = nxt("tf", 3)
                        c.op("act", lambda h, t=t, wf=wf: h.activation(out=TF[wf][0:4, 0:128], in_=psT[t][0:4, 0:128], func=AF.Copy), reads=[PTB[t]], writes=[TFB[wf]])
                        c.dma("sp", lambda h, q=q, wf=wf: h.dma_start(out=o_swv_s[l, q, 124:128, :], in_=TF[wf][0:4, 0:128]), reads=[TFB[wf]], writes=[OUTB])
        pm, psm, pb = win_tile(l, [(G_A, 16)])
        ev2("act", lambda h, d, s, a, b: h.activation(out=d, in_=s, func=AF.Copy), a_ext[0:16, :], pm, psm, pb, writes=[AEB])
        for p in range(2):
            pm, psm, pb = win_tile(l, [(G_K + 128 * p, 128)])
            ev2("act", lambda h, d, s, a, b: h.activation(out=d, in_=s, func=AF.Copy), kT_bf[:, p, :], pm, psm, pb, writes=[KTB[p]])
        for kv in range(2):
            pm, psm, pb = win_tile(l, [(S_K + 64 * kv, 64), (S_K + 64 * kv, 64)])
            qf, qfb = WF[0], WFB[0]; sq, sqb = WH[0], WHB[0]; rs, rsb = WF[1], WFB[1]
            ev2("act", lambda h, d, s, a, b: h.activation(out=d, in_=s, func=AF.Copy), qf, pm, psm, pb, writes=[qfb])
            headnorm(qf, qfb, sq, sqb, rs, rsb, bdiag, 64.0)
            c.op("dve", lambda h, kv=kv: h.scalar_tensor_tensor(out=kn[:, kv, :], in0=qf[:, 0:NT], scalar=gsk[:, l:l + 1], in1=rs[:, 0:NT], op0=ALU.mult, op1=ALU.mult), reads=[qfb, rsb, PB, PB0], writes=[KNB[kv]])
            for i in [7] + list(range(8, 12)):
                t0, m = TT[i]
                t = nxt("pt", 3)
                pv = psT[t][:].bitcast(BF16)
                c.op("pe", lambda h, kv=kv, t0=t0, m=m, pv=pv: h.transpose(out=pv[0:m, 0:128], in_=kn[:, kv, t0:t0 + m], identity=ident[:]), reads=[KNB[kv], CB], writes=[PTB[t]])
                if i == 7:
                    c.op("dve", lambda h, kv=kv, pv=pv: h.tensor_copy(out=pay[:, 258 + 64 * kv:258 + 64 * kv + 64], in_=pv[:, 0:64]), reads=[PTB[t]], writes=[PYB])
                else:
                    q = i - 8
                    wf = nxt("tf", 3)
                    c.op("dve", lambda h, pv=pv, wf=wf: h.tensor_copy(out=TF[wf][0:4, 0:64], in_=pv[0:4, 0:64]), reads=[PTB[t]], writes=[TFB[wf]])
                    c.dma("sp", lambda h, q=q, kv=kv, wf=wf: h.dma_start(out=o_swk_s[l, q, 124:128, 64 * kv:64 * kv + 64], in_=TF[wf][0:4, 0:64]), reads=[TFB[wf]], writes=[OUTB])
        c.dma("sp", lambda h: h.dma_start(out=o_swk_p[l], in_=pay[:, 258:386]), reads=[PYB], writes=[OUTB])
        c.dma("sp", lambda h: h.dma_start(out=o_swv_p[l], in_=pay[:, 386:514]), reads=[PYB], writes=[OUTB])
        for q in range(NSQ):
            c.dma("sp", lambda h, q=q: h.dma_start(out=o_swk_s[l, q, 0:124, :], in_=c_swk[l, q, 4:128, :]), writes=[OUTB])
            c.dma("sp", lambda h, q=q: h.dma_start(out=o_swv_s[l, q, 0:124, :], in_=c_swv[l, q, 4:128, :]), writes=[OUTB])

    def headnorm(qf, qfb, sq, sqb, rs, rsb, onesm, hd):
        c.op("act", lambda h: h.activation(out=sq[:, 0:NT], in_=qf[:, 0:NT], func=AF.Square), reads=[qfb], writes=[sqb])
        for (a, b) in ((0, 512), (512, 1024), (1024, NT)):
            t = nxt("pt", 3)
            c.op("pe", lambda h, a=a, b=b, t=t: h.matmul(psT[t][:, 0:b - a], lhsT=onesm[:], rhs=sq[:, a:b], start=True, stop=True), reads=[sqb, CB], writes=[PTB[t]])
            c.op("act", lambda h, a=a, b=b, t=t: h.activation(out=rs[:, a:b], in_=psT[t][:, 0:b - a], func=AF.Ln, bias=EPS, scale=1.0 / hd), reads=[PTB[t]], writes=[rsb])
        c.op("act", lambda h: h.activation(out=rs[:, 0:NT], in_=rs[:, 0:NT], func=AF.Exp, scale=-0.5), reads=[rsb], writes=[rsb])

    def sp_tile(l, i):
        t0, m = TT[i]
        t = nxt("pt", 3)
        c.op("pe", lambda h: h.matmul(psT[t][0:m, 0:256], lhsT=a_ext[:, t0:t0 + m], rhs=wupb[:, l, :], start=True, stop=True), reads=[AEB, PB, PB0], writes=[PTB[t]])
        wf = nxt("tf", 3)
        sp = TF[wf]
        c.op("act", lambda h: h.activation(out=sp[0:m, 0:256], in_=psT[t][0:m, 0:256], func=AF.Exp, scale=-1.0), reads=[PTB[t]], writes=[TFB[wf]])
        c.op("act", lambda h: h.activation(out=sp[0:m, 0:256], in_=sp[0:m, 0:256], func=AF.Ln, bias=1.0), reads=[TFB[wf]], writes=[TFB[wf]])
        return sp, TFB[wf]

    def gla_chain(l):
        for p in range(2):
            c.op("dve", lambda h, p=p: h.memset(S_run[:, p, :], 0.0), writes=[SRB[p]])
        c.op("dve", lambda h: h.memset(eG[:], 1.0), writes=[EGB])
        for i, (t0, m) in enumerate(TT):
            sp, spb = sp_tile(l, i)
            for p in range(2):
                t = nxt("pt", 3)
                c.op("pe", lambda h, t=t, p=p: h.matmul(psT[t][:, 0:m], lhsT=sp[0:m, 128 * p:128 * p + 128], rhs=triU[0:m, 0:m], start=True, stop=True), reads=[spb, CB], writes=[PTB[t]])
                nb, nbb = smcol()
                c.op("dve", lambda h, t=t, nb=nb: h.tensor_scalar(out=nb, in0=psT[t][:, m - 1:m], scalar1=-1.0 / 16, scalar2=None, op0=ALU.mult), reads=[PTB[t]], writes=[nbb])
                wf = nxt("tf", 3)
                et = TF[wf]
                c.op("act", lambda h, t=t, nb=nb, et=et: h.activation(out=et[:, 0:m], in_=psT[t][:, 0:m], func=AF.Exp, scale=1.0 / 16, bias=nb), reads=[PTB[t], nbb], writes=[TFB[wf]])
                dt_, dtb = smcol()
                c.op("act", lambda h, nb=nb, dt_=dt_: h.activation(out=dt_, in_=nb, func=AF.Exp), reads=[nbb], writes=[dtb])
                wh = nxt("th", 2)
                c.op("dve", lambda h, et=et, wh=wh, p=p: h.tensor_tensor(out=TH[wh][:, 0:m], in0=kT_bf[:, p, t0:t0 + m], in1=et[:, 0:m], op=ALU.mult), reads=[KTB[p], TFB[wf]], writes=[THB[wh]])
                t2 = nxt("pt", 3)
                pv = psT[t2][:].bitcast(BF16)
                c.op("pe", lambda h, wh=wh, pv=pv: h.transpose(out=pv[0:m, 0:128], in_=TH[wh][:, 0:m], identity=ident[:]), reads=[THB[wh], CB], writes=[PTB[t2]])
                wh2 = wh
                c.op("act", lambda h, wh2=wh2, pv=pv: h.activation(out=TH[wh2][0:m, 512:640], in_=pv[0:m, 0:128], func=AF.Copy), reads=[PTB[t2]], writes=[THB[wh2]])
                t3 = nxt("pt", 3)

                def mm(h, t3=t3, wh2=wh2, p=p):
                    r = None
                    for hh in range(2):
                        hd = 2 * p + hh
                        r = h.matmul(psT[t3][64 * hh:64 * hh + 64, 0:128], lhsT=TH[wh2][0:m, 512 + 64 * hh:512 + 64 * hh + 64], rhs=v_tm[0:m, i, 128 * hd:128 * hd + 128], start=True, stop=True)
                    return r
                c.op("pe", mm, reads=[THB[wh2], VB[i]], writes=[PTB[t3]])
                if i < 8:
                    c.op("act", lambda h, p=p: h.copy(out=Sloc[:, p, i, :], in_=S_run[:, p, :]), reads=[SRB[p]], writes=[SLB[p][i]])
                    c.op("dve", lambda h, p=p, t3=t3, dt_=dt_: h.scalar_tensor_tensor(out=S_run[:, p, :], in0=S_run[:, p, :], scalar=dt_, in1=psT[t3][:, 0:128], op0=ALU.mult, op1=ALU.add), reads=[SRB[p], dtb, PTB[t3]], writes=[SRB[p]])
                    c.op("dve", lambda h, p=p, dt_=dt_: h.tensor_tensor(out=eG[:, p, i + 1:i + 2], in0=eG[:, p, i:i + 1], in1=dt_, op=ALU.mult), reads=[EGB, dtb], writes=[EGB])
                else:
                    q = i - 8
                    wf2 = nxt("tf", 3)
                    c.op("dve", lambda h, p=p, q=q, t3=t3, dt_=dt_, wf2=wf2: h.scalar_tensor_tensor(out=TF[wf2][:, 0:128], in0=S_st[:, p, 1 + q, :], scalar=dt_, in1=psT[t3][:, 0:128], op0=ALU.mult, op1=ALU.add), reads=[SSB, dtb, PTB[t3]], writes=[TFB[wf2]])
                    c.dma("sp", lambda h, p=p, q=q, wf2=wf2: h.dma_start(out=o_gla_s[l, q, 128 * p:128 * p + 128, :], in_=TF[wf2][:, 0:128]), reads=[TFB[wf2]], writes=[OUTB])
        for p in range(2):
            c.op("dve", lambda h, p=p: h.tensor_copy(out=pay[:, 128 * p:128 * p + 128], in_=S_run[:, p, :]), reads=[SRB[p]], writes=[PYB])
            c.op("dve", lambda h, p=p: h.tensor_copy(out=pay[:, 256 + p:257 + p], in_=eG[:, p, 8:9]), reads=[EGB], writes=[PYB])

    def exchange(l):
        c.op("dve", lambda h: h.tensor_copy(out=pay[:, 514:522].rearrange("p (a b) -> p a b", a=4), in_=ulast[:, :, 0, :]), reads=[FXB], writes=[PYB])
        c.dma("sp", lambda h: h.dma_start(out=agin[l].ap(), in_=pay[:]), reads=[PYB], writes=[AGB[l][0]])
        c.op("pool", lambda h: h.collective_compute("AllGather", ALU.bypass, replica_groups=[[0, 1, 2, 3], [4, 5, 6, 7]],
             ins=[agin[l].ap().opt()], outs=[agout[l].ap().opt()]), reads=[AGB[l][0]], writes=[AGB[l][1]])

    def kprev_make(src_f32_ap, srcb, seg):
        for kv in range(2):
            wh = nxt("th", 2)
            for hh in range(2):
                c.op("dve", lambda h, wh=wh, kv=kv, hh=hh: h.tensor_copy(out=TH[wh][:, 64 * hh:64 * hh + 64], in_=src_f32_ap[:, 64 * kv:64 * kv + 64]), reads=[srcb], writes=[THB[wh]])
            t = nxt("pt", 3)
            pv = psT[t][:].bitcast(BF16)
            c.op("pe", lambda h, wh=wh, pv=pv: h.transpose(out=pv[:, 0:128], in_=TH[wh][:, 0:128], identity=ident[:]), reads=[THB[wh], CB], writes=[PTB[t]])
            c.op("act", lambda h, kv=kv, pv=pv: h.activation(out=kprev[:, kv, seg, :], in_=pv[:, 0:128], func=AF.Copy), reads=[PTB[t]], writes=[KPB[seg]])

    def combine(l):
        for p in range(2):
            c.op("dve", lambda h, p=p: h.memset(S_st[:, p, 0, :], 0.0), writes=[SSB])
        acc, accb = WF[2], WFB[2]
        c.op("dve", lambda h: h.memset(acc[:, 0:272], 0.0), writes=[accb])
        for r in range(4):
            wf = nxt("tf", 3)
            g, gb = TF[wf], TFB[wf]
            c.dma("sp", lambda h, r=r, g=g: h.dma_start(out=g[:, 0:PAYC], in_=agout[l].ap()[128 * r:128 * r + 128, :]), reads=[AGB[l][1]], writes=[gb])
            dcol, dcb = smcol(); dcol2, dcb2 = smcol()
            for p, dc, db in ((0, dcol, dcb), (1, dcol2, dcb2)):
                c.op("dve", lambda h, p=p, dc=dc, g=g: h.tensor_scalar(out=dc, in0=g[:, 256 + p:257 + p], scalar1=-1.0, scalar2=cm[:, r:r + 1], op0=ALU.add, op1=ALU.mult), reads=[gb, PB, PB0], writes=[db])
                c.op("dve", lambda h, dc=dc: h.tensor_scalar(out=dc, in0=dc, scalar1=1.0, scalar2=None, op0=ALU.add), reads=[db], writes=[db])
                c.op("dve", lambda h, p=p, g=g: h.tensor_scalar(out=g[:, 128 * p:128 * p + 128], in0=g[:, 128 * p:128 * p + 128], scalar1=cm[:, r:r + 1], scalar2=None, op0=ALU.mult), reads=[gb, PB, PB0], writes=[gb])
                c.op("dve", lambda h, p=p, dc=dc, g=g: h.scalar_tensor_tensor(out=S_st[:, p, 0, :], in0=S_st[:, p, 0, :], scalar=dc, in1=g[:, 128 * p:128 * p + 128], op0=ALU.mult, op1=ALU.add), reads=[SSB, db, gb], writes=[SSB])
            c.op("dve", lambda h, g=g: h.scalar_tensor_tensor(out=acc[:, 0:264], in0=g[:, 258:522], scalar=cm[:, 4 + r:5 + r], in1=acc[:, 0:264], op0=ALU.mult, op1=ALU.add), reads=[gb, PB, accb], writes=[accb])
        kprev_make(acc[:, 0:128], accb, 0)
        c.op("dve", lambda h: h.tensor_copy(out=vprev[:, 0, :], in_=acc[:, 128:256]), reads=[accb], writes=[VPB[0]])
        c.op("dve", lambda h: h.tensor_copy(out=cprev[:, :, 0, :], in_=acc[:, 256:264].rearrange("p (a b) -> p a b", a=4)), reads=[accb], writes=[CPB, PB0])

    def sample_states(l):
        stg, stgb = WF[2], WFB[2]
        c.dma("sp", lambda h: h.dma_start(out=stg[:, 0:512].rearrange("p (q d) -> p q d", q=4), in_=c_swk[l].rearrange("q s d -> s q d")), writes=[stgb])
        c.dma("sp", lambda h: h.dma_start(out=stg[:, 512:1024].rearrange("p (q d) -> p q d", q=4), in_=c_swv[l].rearrange("q s d -> s q d")), writes=[stgb])
        for q in range(NSQ):
            for j in range(2):
                c.dma("sp", lambda h, q=q, j=j: h.dma_start(out=cprev[:, :, 1 + q, j], in_=st_conv[l, q, j:j + 1, :].rearrange("o (ct p) -> p (o ct)", p=128), **NCg), writes=[CPB, PB0])
            c.dma("sp", lambda h, q=q: h.dma_start(out=S_st[:, :, 1 + q, :], in_=st_gla[l, q].rearrange("(pr hh) d v -> (hh d) pr v", pr=2)), writes=[SSB])
        for q in range(NSQ):
            kprev_make(stg[:, 128 * q:128 * q + 128], stgb, 1 + q)
            c.op("dve", lambda h, q=q: h.tensor_copy(out=vprev[:, 1 + q, :], in_=stg[:, 512 + 128 * q:512 + 128 * q + 128]), reads=[stgb], writes=[VPB[1 + q]])

    def mem_group(l, mbuf):
        isq = 1.0 / np.sqrt(128.0)

        def attend(hd, qn_ap, n, c0, dst_fn, qb):
            def sc(h):
                h.matmul(psA[:, 0:n], lhsT=memK[:, hd, 0:128], rhs=qn_ap, start=True, stop=True)
                return h.matmul(psA[:, 512:512 + n], lhsT=memK[:, hd, 128:256], rhs=qn_ap, start=True, stop=True)
            c.op("pe", sc, reads=[MKB] + qb, writes=[PAB[0]])
            wh = nxt("th", 2)
            pb_ = TH[wh]
            for mt in range(2):
                c.op("act", lambda h, mt=mt, pb_=pb_: h.activation(out=pb_[:, 512 * mt:512 * mt + n], in_=psA[:, 512 * mt:512 * mt + n], func=AF.Exp, scale=isq), reads=[PAB[0]], writes=[THB[wh]])

            def pv(h):
                for mt in range(2):
                    h.matmul(psB[:, 0:n], lhsT=memV[:, mt, 128 * hd:128 * hd + 128], rhs=pb_[:, 512 * mt:512 * mt + n], start=(mt == 0), stop=(mt == 1))
                r = None
                for mt in range(2):
                    r = h.matmul(psB[:, 512:512 + n], lhsT=onesb[:], rhs=pb_[:, 512 * mt:512 * mt + n], start=(mt == 0), stop=(mt == 1))
                return r
            c.op("pe", pv, reads=[MVB, THB[wh], CB], writes=[PAB[1]])
            wf = nxt("tf", 3)
            c.op("dve", lambda h, wf=wf: h.reciprocal(out=TF[wf][:, 0:n], in_=psB[:, 512:512 + n]), reads=[PAB[1]], writes=[TFB[wf]])
            dst_fn(wf)

        for hd in range(4):
            pm, psm, pb = win_tile(l, [(M_Q + 128 * hd, 128)])
            qf, qfb = WF[0], WFB[0]; sq, sqb = WH[0], WHB[0]; rs, rsb = WF[1], WFB[1]
            qn_, qnb = WH[1], WHB[1]; szt, szb = WF[2], WFB[2]
            ev2("act", lambda h, d, s, a, b: h.activation(out=d, in_=s, func=AF.Copy), qf, pm, psm, pb, writes=[qfb])
            headnorm(qf, qfb, sq, sqb, rs, rsb, onesb, 128.0)
            c.op("dve", lambda h: h.scalar_tensor_tensor(out=qn_[:, 0:NT], in0=qf[:, 0:NT], scalar=gmq[:, l:l + 1], in1=rs[:, 0:NT], op0=ALU.mult, op1=ALU.mult), reads=[qfb, rsb, PB, PB0], writes=[qnb])
            pm, psm, pb = win_tile(l, [(M_Z + 128 * hd, 128)])
            ev2("act", lambda h, d, s, a, b: h.activation(out=d, in_=s, func=AF.Silu), szt, pm, psm, pb, writes=[szb])
            c.op("act", lambda h, hd=hd: h.copy(out=qn_s[:, hd, :], in_=qn_[:, NP:NT]), reads=[qnb], writes=[QSB])
            c.op("act", lambda h, hd=hd: h.copy(out=sz_s[:, hd, :], in_=szt[:, NP:NT]), reads=[szb], writes=[QSB])
            for half in range(2):
                c0 = 512 * half

                def dst(wf, c0=c0, hd=hd):
                    c.op("dve", lambda h: h.tensor_tensor(out=TF[wf][:, 0:512], in0=psB[:, 0:512], in1=TF[wf][:, 0:512], op=ALU.mult), reads=[PAB[1], TFB[wf]], writes=[TFB[wf]])
                    c.op("dve", lambda h: h.tensor_tensor(out=mixT[:, hd, c0:c0 + 512], in0=TF[wf][:, 0:512], in1=szt[:, c0:c0 + 512], op=ALU.mult), reads=[TFB[wf], szb], writes=[MB[mbuf][hd]])
                attend(hd, qn_[:, c0:c0 + 512], 512, c0, dst, [qnb])
        for q in range(NSQ):
            kst, kstb = WF[0], WFB[0]
            c.dma("sp", lambda h, q=q: h.dma_start(out=kst[:, 0:1024].rearrange("p (mt f) -> p mt f", mt=2), in_=c_mk[l, q].rearrange("(mt p) f -> p mt f", p=128)), writes=[kstb])
            for mt in range(2):
                for hd in range(4):
                    wh = nxt("th", 2)
                    c.op("dve", lambda h, wh=wh, mt=mt, hd=hd: h.tensor_copy(out=TH[wh][:, 0:128], in_=kst[:, 512 * mt + 128 * hd:512 * mt + 128 * hd + 128]), reads=[kstb], writes=[THB[wh]])
                    t = nxt("pt", 3)
                    pv = psT[t][:].bitcast(BF16)
                    c.op("pe", lambda h, wh=wh, pv=pv: h.transpose(out=pv[:, 0:128], in_=TH[wh][:, 0:128], identity=ident[:]), reads=[THB[wh], CB], writes=[PTB[t]])
                    c.op("act", lambda h, pv=pv, hd=hd, mt=mt: h.activation(out=memK[:, hd, 128 * mt:128 * mt + 128], in_=pv[:, 0:128], func=AF.Copy), reads=[PTB[t]], writes=[MKB])
            vst, vstb = WF[1], WFB[1]
            c.dma("sp", lambda h, q=q: h.dma_start(out=vst[:, 0:1024].rearrange("p (mt f) -> p mt f", mt=2), in_=c_mv[l, q].rearrange("(mt p) f -> p mt f", p=128)), writes=[vstb])
            c.op("dve", lambda h: h.tensor_copy(out=WH[2][:, 0:1024], in_=vst[:, 0:1024]), reads=[vstb], writes=[MVB])
            for hd in range(4):
                def dst(wf, hd=hd, q=q):
                    c.op("dve", lambda h: h.tensor_tensor(out=TF[wf][:, 0:4], in0=psB[:, 0:4], in1=TF[wf][:, 0:4], op=ALU.mult), reads=[PAB[1], TFB[wf]], writes=[TFB[wf]])
                    c.op("dve", lambda h: h.tensor_tensor(out=mixT[:, hd, NP + 4 * q:NP + 4 * q + 4], in0=TF[wf][:, 0:4], in1=sz_s[:, hd, 4 * q:4 * q + 4], op=ALU.mult), reads=[TFB[wf], QSB], writes=[MB[mbuf][hd]])
                attend(hd, qn_s[:, hd, 4 * q:4 * q + 4], 4, 0, dst, [QSB])

    def swa_group(l, mbuf):
        for i4 in range(4):
            kv = i4 // 2
            pm, psm, pb = win_tile(l, [(S_Q + 128 * i4, 128)])
            qf, qfb = WF[0], WFB[0]; sq, sqb = WH[0], WHB[0]; rs, rsb = WF[1], WFB[1]
            qn_, qnb = WH[1], WHB[1]; szt, szb = WH[2], WHB[2]; osw, oswb = WF[2], WFB[2]
            ev2("act", lambda h, d, s, a, b: h.activation(out=d, in_=s, func=AF.Copy), qf, pm, psm, pb, writes=[qfb])
            headnorm(qf, qfb, sq, sqb, rs, rsb, bdiag, 64.0)
            qn1, qn1b = WH[3], WHB[3]
            c.op("dve", lambda h: h.scalar_tensor_tensor(out=qn_[0:64, 0:NT], in0=qf[0:64, 0:NT], scalar=gsq[0:64, l:l + 1], in1=rs[0:64, 0:NT], op0=ALU.mult, op1=ALU.mult), reads=[qfb, rsb, PB, PB0], writes=[qnb])
            c.op("dve", lambda h: h.memset(qn_[64:128, 0:NT], 0.0), writes=[qnb])
            c.op("dve", lambda h: h.scalar_tensor_tensor(out=qn1[64:128, 0:NT], in0=qf[64:128, 0:NT], scalar=gsq[64:128, l:l + 1], in1=rs[64:128, 0:NT], op0=ALU.mult, op1=ALU.mult), reads=[qfb, rsb, PB, PB0], writes=[qn1b])
            c.op("dve", lambda h: h.memset(qn1[0:64, 0:NT], 0.0), writes=[qn1b])
            qh = [qn_, qn1]
            pm, psm, pb = win_tile(l, [(S_Z + 128 * i4, 128)])
            ev2("act", lambda h, d, s, a, b: h.activation(out=d, in_=s, func=AF.Silu), szt, pm, psm, pb, writes=[szb])
            def stage1(i):
                t0, m = TT[i]
                if i == 0 or i >= 8:
                    seg = 0 if i == 0 else i - 7
                    kp = kprev[:, kv, seg, :]; kpb = KPB[seg]
                    vp = vprev[:, seg, 64 * kv:64 * kv + 64]; vpb = VPB[seg]
                else:
                    kp = kn[:, kv, t0 - 128:t0]; kpb = KNB[kv]
                    vp = vs_tm[:, i - 1, 64 * kv:64 * kv + 64]; vpb = VSB[i - 1]
                msk = swm1 if i == 0 else swm
                t = nxt("pt", 3)
                scv = psT[t][:].rearrange("p (hh b n) -> p hh b n", hh=2, b=2)

                def sc(h, scv=scv, kp=kp, t0=t0, m=m, qh=qh):
                    r = None
                    for hh in range(2):
                        lo = 64 * hh
                        h.matmul(scv[:, hh, 0, 0:m], lhsT=kp, rhs=qh[hh][:, t0:t0 + m], start=True, stop=True)
                        r = h.matmul(scv[0:m, hh, 1, 0:m], lhsT=kn[:, kv, t0:t0 + m], rhs=qh[hh][:, t0:t0 + m], start=True, stop=True)
                    return r
                c.op("pe", sc, reads=[kpb, KNB[kv], qnb, qn1b], writes=[PTB[t]])
                wh = nxt("th", 2)
                pbv = TH[wh][:, 0:512].rearrange("p (hh b n) -> p hh b n", hh=2, b=2)
                WHB_wh = THB[wh]
                if m == 128:
                    c.op("act", lambda h, t=t, wh=wh: h.activation(out=TH[wh][:, 0:512], in_=psT[t][:, 0:512], func=AF.Exp, scale=0.125), reads=[PTB[t]], writes=[WHB_wh])
                    c.op(SWA_MASK_ENG, lambda h, wh=wh, msk=msk: h.tensor_tensor(out=TH[wh][:, 0:512], in0=TH[wh][:, 0:512], in1=msk[:].rearrange("p a b n -> p (a b n)"), op=ALU.mult), reads=[WHB_wh, CB], writes=[WHB_wh])
                else:
                    c.op("act", lambda h, scv=scv, pbv=pbv, m=m: h.activation(out=pbv[:, :, 0, 0:m], in_=scv[:, :, 0, 0:m], func=AF.Exp, scale=0.125), reads=[PTB[t]], writes=[WHB_wh])
                    c.op("act", lambda h, scv=scv, pbv=pbv, m=m: h.activation(out=pbv[0:m, :, 1, 0:m], in_=scv[0:m, :, 1, 0:m], func=AF.Exp, scale=0.125), reads=[PTB[t]], writes=[WHB_wh])
                    c.op(SWA_MASK_ENG, lambda h, pbv=pbv, msk=msk, m=m: h.tensor_tensor(out=pbv[:, :, 0, 0:m], in0=pbv[:, :, 0, 0:m], in1=msk[:, :, 0, 0:m], op=ALU.mult), reads=[WHB_wh, CB], writes=[WHB_wh])
                    c.op(SWA_MASK_ENG, lambda h, pbv=pbv, msk=msk, m=m: h.tensor_tensor(out=pbv[0:m, :, 1, 0:m], in0=pbv[0:m, :, 1, 0:m], in1=msk[0:m, :, 1, 0:m], op=ALU.mult), reads=[WHB_wh, CB], writes=[WHB_wh])
                return (t0, m, pbv, WHB_wh, vp, vpb)

            def stage2(i, carry):
                t0, m, pbv, WHB_wh, vp, vpb = carry
                t2 = nxt("pt", 3)

                def pvm(h, t2=t2, pbv=pbv, vp=vp, i=i, m=m):
                    r = None
                    for hh in range(2):
                        lo = 64 * hh
                        h.matmul(psT[t2][lo:lo + 64, 0:m], lhsT=vp, rhs=pbv[:, hh, 0, 0:m], start=True, stop=False)
                        h.matmul(psT[t2][lo:lo + 64, 0:m], lhsT=vs_tm[0:m, i, 64 * kv:64 * kv + 64], rhs=pbv[0:m, hh, 1, 0:m], start=False, stop=True)
                        h.matmul(psT[t2][lo:lo + 64, 128:128 + m], lhsT=onesb[:, 0:64], rhs=pbv[:, hh, 0, 0:m], start=True, stop=False)
                        r = h.matmul(psT[t2][lo:lo + 64, 128:128 + m], lhsT=onesb[0:m, 0:64], rhs=pbv[0:m, hh, 1, 0:m], start=False, stop=True)
                    return r
                c.op("pe", pvm, reads=[WHB_wh, vpb, VSB[i], CB], writes=[PTB[t2]])
                wf = nxt("tf", 3)
                rr = TF[wf]
                c.op("dve", lambda h, t2=t2, rr=rr, m=m: h.tensor_scalar(out=rr[:, 0:m], in0=psT[t2][:, 128:128 + m], scalar1=esq[:, l, i4:i4 + 1], scalar2=None, op0=ALU.add), reads=[PTB[t2], PB, PB0], writes=[TFB[wf]])
                c.op("dve", lambda h, rr=rr, m=m: h.reciprocal(out=rr[:, 0:m], in_=rr[:, 0:m]), reads=[TFB[wf]], writes=[TFB[wf]])
                c.op("dve", lambda h, t2=t2, rr=rr, m=m, t0=t0: h.tensor_tensor(out=osw[:, t0:t0 + m], in0=psT[t2][:, 0:m], in1=rr[:, 0:m], op=ALU.mult), reads=[PTB[t2], TFB[wf]], writes=[oswb])
            carry = stage1(0)
            for i in range(len(TT)):
                nxt_carry = stage1(i + 1) if i + 1 < len(TT) else None
                stage2(i, carry)
                carry = nxt_carry
            c.op("dve", lambda h, i4=i4: h.tensor_tensor(out=mixT[:, i4, :], in0=osw[:, 0:NT], in1=szt[:, 0:NT], op=ALU.mult), reads=[oswb, szb], writes=[MB[mbuf][i4]])

    def gla_group(l, mbuf):
        for p in range(2):
            wf = nxt("tf", 3)
            c.op("dve", lambda h, p=p, wf=wf: h.scalar_tensor_tensor(out=TF[wf][:, 0:128], in0=S_st[:, p, 0, :], scalar=eG[:, p, 8:9], in1=S_run[:, p, :], op0=ALU.mult, op1=ALU.add), reads=[SSB, EGB, SRB[p]], writes=[TFB[wf]])
            c.dma("sp", lambda h, p=p, wf=wf: h.dma_start(out=o_gla_p[l, 128 * p:128 * p + 128, :], in_=TF[wf][:, 0:128]), reads=[TFB[wf]], writes=[OUTB])
        for p in range(2):
            pm, psm, pb = win_tile(l, [(G_Q + 128 * p, 128)])
            qraw, qrb = WH[0], WHB[0]
            ev2("act", lambda h, d, s, a, b: h.activation(out=d, in_=s, func=AF.Copy, scale=0.125), qraw, pm, psm, pb, writes=[qrb])
            qdec, qdb = WH[1], WHB[1]; kdec, kdb = WH[2], WHB[2]
            for i, (t0, m) in enumerate(TT):
                sp, spb = sp_tile(l, i)
                t = nxt("pt", 3)
                c.op("pe", lambda h, t=t, sp=sp, m=m, p=p: h.matmul(psT[t][:, 0:m], lhsT=sp[0:m, 128 * p:128 * p + 128], rhs=triU[0:m, 0:m], start=True, stop=True), reads=[spb, CB], writes=[PTB[t]])
                wf = nxt("tf", 3)
                e1 = TF[wf]
                c.op("act", lambda h, t=t, e1=e1, m=m: h.activation(out=e1[:, 0:m], in_=psT[t][:, 0:m], func=AF.Exp, scale=-1.0 / 16), reads=[PTB[t]], writes=[TFB[wf]])
                c.op("act", lambda h, t=t, e1=e1, m=m: h.activation(out=e1[:, 128:128 + m], in_=psT[t][:, 0:m], func=AF.Exp, scale=1.0 / 16), reads=[PTB[t]], writes=[TFB[wf]])
                c.op("dve", lambda h, e1=e1, t0=t0, m=m: h.tensor_tensor(out=qdec[:, t0:t0 + m], in0=qraw[:, t0:t0 + m], in1=e1[:, 0:m], op=ALU.mult), reads=[qrb, TFB[wf]], writes=[qdb])
                c.op("dve", lambda h, e1=e1, t0=t0, m=m, p=p: h.tensor_tensor(out=kdec[:, t0:t0 + m], in0=kT_bf[:, p, t0:t0 + m], in1=e1[:, 128:128 + m], op=ALU.mult), reads=[KTB[p], TFB[wf]], writes=[kdb])
            for hh in range(2):
                hd = 2 * p + hh
                lo = 64 * hh
                osb, osbb = WF[2], WFB[2]
                def g1(i):
                    t0, m = TT[i]
                    th = nxt("th", 2)
                    sc_, scb = TH[th], THB[th]
                    if i < 8:
                        c.op("dve", lambda h, i=i, sc_=sc_, lo=lo, p=p: h.scalar_tensor_tensor(out=sc_[lo:lo + 64, 0:128], in0=S_st[lo:lo + 64, p, 0, :], scalar=eG[lo:lo + 64, p, i:i + 1], in1=Sloc[lo:lo + 64, p, i, :], op0=ALU.mult, op1=ALU.add), reads=[SSB, EGB, SLB[p][i]], writes=[scb])
                    else:
                        c.op("dve", lambda h, i=i, sc_=sc_, lo=lo, p=p: h.tensor_copy(out=sc_[lo:lo + 64, 0:128], in_=S_st[lo:lo + 64, p, i - 7, :]), reads=[SSB], writes=[scb])
                    c.op("dve", lambda h, sc_=sc_, lo=lo: h.memset(sc_[64 - lo:128 - lo, 0:128], 0.0), writes=[scb])
                    t = nxt("pt", 3)
                    c.op("pe", lambda h, t=t, t0=t0, m=m, lo=lo: h.matmul(psT[t][0:m, 0:m], lhsT=kdec[lo:lo + 64, t0:t0 + m], rhs=qdec[lo:lo + 64, t0:t0 + m], start=True, stop=True), reads=[kdb, qdb], writes=[PTB[t]])
                    c.op("dve", lambda h, t=t, m=m, sc_=sc_: h.tensor_tensor(out=sc_[0:m, 128:128 + m], in0=psT[t][0:m, 0:m], in1=triU[0:m, 0:m], op=ALU.mult), reads=[PTB[t], CB], writes=[scb])
                    return (t0, m, sc_, scb)

                def g2(i, carry):
                    t0, m, sc_, scb = carry
                    t2 = nxt("pt", 3)

                    def om(h, t2=t2, i=i, t0=t0, m=m, sc_=sc_, lo=lo, hd=hd):
                        h.matmul(psT[t2][:, 0:m], lhsT=v_tm[0:m, i, 128 * hd:128 * hd + 128], rhs=sc_[0:m, 128:128 + m], start=True, stop=False)
                        return h.matmul(psT[t2][:, 0:m], lhsT=sc_[:, 0:128], rhs=qdec[:, t0:t0 + m], start=False, stop=True)
                    c.op("pe", om, reads=[VB[i], scb, qdb], writes=[PTB[t2]])
                    c.op("act", lambda h, t2=t2, t0=t0, m=m: h.activation(out=osb[:, t0:t0 + m], in_=psT[t2][:, 0:m], func=AF.Copy), reads=[PTB[t2]], writes=[osbb])
                carry = g1(0)
                for i in range(len(TT)):
                    nc_ = g1(i + 1) if i + 1 < len(TT) else None
                    g2(i, carry)
                    carry = nc_
                sq, sqb = WH[3], WHB[3]; rs, rsb = WF[0], WFB[0]
                headnorm(osb, osbb, sq, sqb, rs, rsb, onesb, 128.0)
                c.op("dve", lambda h: h.scalar_tensor_tensor(out=osb[:, 0:NT], in0=osb[:, 0:NT], scalar=gov[:, l:l + 1], in1=rs[:, 0:NT], op0=ALU.mult, op1=ALU.mult), reads=[osbb, rsb, PB, PB0], writes=[osbb])
                pm, psm, pb = win_tile(l, [(G_Z + 128 * hd, 128)])
                szt, szb = WF[1], WFB[1]
                ev2("act", lambda h, d, s, a, b: h.activation(out=d, in_=s, func=AF.Silu), szt, pm, psm, pb, writes=[szb])
                c.op("dve", lambda h, hd=hd: h.tensor_tensor(out=mixT[:, hd, :], in0=osb[:, 0:NT], in1=szt[:, 0:NT], op=ALU.mult), reads=[osbb, szb], writes=[MB[mbuf][hd]])

    try:
        stage(1)
        for l in range(2):
            tiles = [(x_res[:, i, :], 128, XB[i], 128 * i, [HB]) for i in range(2, 8)]
            tiles.append((x_s[:], NST, XSB, NP, [HB]))
            tiles += [(x_res[:, i, :], 128, XB[i], 128 * i, [HB, HBm]) for i in range(2)]

            def mk_a(n):
                sa, m, rb, col0, hb = tiles[n]
                return lambda: norm_a(sa, m, rb, n % 2)

            def mk_b(n):
                sa, m, rb, col0, hb = tiles[n]
                return lambda: norm_b(m, gT, l, col0, n % 2, hb)
            xt = [mk_a(0)]
            for n in range(7):
                xt.append((lambda n=n: (mk_a(n + 1)(), mk_b(n)())))
            mem_kv_prompt(l, xt)
            stage(2 + 20 * l)
            mk_a(8)(); mk_b(7)(); mk_b(8)()
            stage(3 + 20 * l)
            sample_states(l)
            stage(4 + 20 * l)
            tm_pass(l)
            stage(5 + 20 * l)
            gla_chain(l)
            stage(6 + 20 * l)
            conv_group(l, 0)
            stage(7 + 20 * l)
            exchange(l)
            stage(8 + 20 * l)
            wout_group(l, 0, 0)
            stage(9 + 20 * l)
            mem_group(l, 0)
            wout_group(l, 3, 0)
            stage(10 + 20 * l)
            combine(l)
            conv_fix(l, 0)
            stage(11 + 20 * l)
            swa_group(l, 0)
            wout_group(l, 2, 0)
            stage(12 + 20 * l)
            gla_group(l, 0)
            wout_group(l, 1, 0)
            stage(13 + 20 * l)
    except _Stop:
        pass
    for i in range(8):
        c.dma("sp", lambda h, i=i: h.dma_start(out=y_p[128 * i:128 * i + 128, :], in_=x_res[:, i, :]), reads=[XB[i]], writes=[OUTB])
    c.dma("sp", lambda h: h.dma_start(out=y_s, in_=x_s[:]), reads=[XSB], writes=[OUTB])
    def fin(h):
        for i, v in enumerate(c.dval):
            if v > 0:
                h.wait_ge(c.dsem[i], v)
    c._push("sp", fin)
    c.finish()
    if wseq_in is None:
        return nc, WSEQ
    return nc


_NC = None


def kernel(x_prompt, x_sample, mem_prompt, state_conv, state_gla, cache_swa_k, cache_swa_v,
           cache_mem_k, cache_mem_v, g_norm, w_in, conv_w, w_gla_a_up, b_gla_a, g_gla_o,
           g_swa_q, g_swa_k, swa_sinks, g_mem, w_mem_kv, g_mem_q, g_mem_k, w_out):
    global _NC
    f = lambda a: np.ascontiguousarray(np.asarray(a, dtype=np.float32))
    x_prompt, x_sample, mem_prompt = f(x_prompt), f(x_sample), f(mem_prompt)
    state_conv, state_gla = f(state_conv), f(state_gla)
    cache_swa_k, cache_swa_v, cache_mem_k, cache_mem_v = f(cache_swa_k), f(cache_swa_v), f(cache_mem_k), f(cache_mem_v)
    shared = {"g_norm": f(g_norm), "w_in": f(w_in), "conv_w": f(conv_w), "w_up": f(w_gla_a_up), "b_a": f(b_gla_a),
              "g_o": f(g_gla_o), "g_sq": f(g_swa_q), "g_sk": f(g_swa_k), "sinks": f(swa_sinks), "g_mem": f(g_mem),
              "w_mkv": f(w_mem_kv), "g_mq": f(g_mem_q), "g_mk": f(g_mem_k), "w_out": f(w_out)}
    in_maps = []
    for cid in range(8):
        b, j = cid // 4, cid % 4
        sq = slice(4 * cid, 4 * cid + 4)
        cmk = np.zeros((1, 16), np.float32)
        cmk[0, 0:4] = [1.0 if r < j else 0.0 for r in range(4)]
        cmk[0, 4:8] = [1.0 if r == j - 1 else 0.0 for r in range(4)]
        cmk[0, 8] = 1.0 if j > 0 else 0.0
        m = dict(shared)
        m.update({
            "xp": f(x_prompt[b, 1024 * j:1024 * j + 1024]), "xs": f(x_sample[sq].reshape(16, D)), "memp": f(mem_prompt[b]),
            "st_conv": f(state_conv[:, sq]), "st_gla": f(state_gla[:, sq]),
            "c_swk": f(cache_swa_k[:, sq].reshape(2, 4, 128, 128)), "c_swv": f(cache_swa_v[:, sq].reshape(2, 4, 128, 128)),
            "c_mk": f(cache_mem_k[:, sq].reshape(2, 4, 256, 512)), "c_mv": f(cache_mem_v[:, sq].reshape(2, 4, 256, 512)),
            "cmask": cmk})
        in_maps.append(m)
    if _NC is None:
        _, _wseq = build()
        _NC = build(_wseq)
    res = run_bass_kernel_spmd(_NC, in_maps, core_ids=list(range(8))).results
    y_prompt = np.zeros((2, 4096, D), np.float32); y_sample = np.zeros((32, 4, D), np.float32)
    conv_p = np.zeros((2, 2, 2, 512), np.float32); gla_p = np.zeros((2, 2, 4, 64, 128), np.float32)
    swk_p = np.zeros((2, 2, 128, 2, 64), np.float32); swv_p = np.zeros((2, 2, 128, 2, 64), np.float32)
    mk_p = np.zeros((2, 2, 256, 4, 128), np.float32); mv_p = np.zeros((2, 2, 256, 4, 128), np.float32)
    conv_s = np.zeros((2, 32, 2, 512), np.float32); gla_s = np.zeros((2, 32, 4, 64, 128), np.float32)
    swk_s = np.zeros((2, 32, 128, 2, 64), np.float32); swv_s = np.zeros((2, 32, 128, 2, 64), np.float32)
    for cid in range(8):
        b, j = cid // 4, cid % 4
        r = res[cid]
        sq = slice(4 * cid, 4 * cid + 4)
        y_prompt[b, 1024 * j:1024 * j + 1024] = r["y_p"].reshape(1024, D)
        y_sample[sq] = r["y_s"].reshape(4, 4, D)
        conv_s[:, sq] = r["conv_s"].reshape(2, 4, 2, 512); gla_s[:, sq] = r["gla_s"].reshape(2, 4, 4, 64, 128)
        swk_s[:, sq] = r["swk_s"].reshape(2, 4, 128, 2, 64); swv_s[:, sq] = r["swv_s"].reshape(2, 4, 128, 2, 64)
        if j == 3:
            conv_p[:, b] = r["conv_p"].reshape(2, 2, 512); gla_p[:, b] = r["gla_p"].reshape(2, 4, 64, 128)
            swk_p[:, b] = r["swk_p"].reshape(2, 128, 2, 64); swv_p[:, b] = r["swv_p"].reshape(2, 128, 2, 64)
        if j == 0:
            mk_p[:, b] = r["mk_p"].reshape(2, 256, 4, 128); mv_p[:, b] = r["mv_p"].reshape(2, 256, 4, 128)
    return (y_prompt, y_sample, conv_p, gla_p, swk_p, swv_p, mk_p, mv_p, conv_s, gla_s, swk_s, swv_s)
```
